# Optimizing a Trainium2 kernel written in Bass

```python
import jax, jax.numpy as jnp
from jax import lax
import numpy as np

D_MODEL = 1024
BATCH = 16
SEQ = 2048
DEPTH = 4

GRID_W = 64
CTX_LEN = 256
N_MIXERS = 2

N_Q_HEADS = 16
N_KV_HEADS = 4
HEAD_DIM = 128
ATTN_WIDTH = N_Q_HEADS * HEAD_DIM
KV_WIDTH = N_KV_HEADS * HEAD_DIM
ATTN_PROJ = 2 * ATTN_WIDTH + 2 * KV_WIDTH
WINDOW = 128
BLOCK = 128
ROPE_BASE = 10000.0

CONV_WIDTH = 2 * D_MODEL
CONV_PROJ = 3 * CONV_WIDTH
CONV_K = 31

EPS = 1e-6

kernel_name = "hybrid_swa_sink_conformer_prefix_dit"


def rmsnorm(x, g):
    xf = x.astype(jnp.float32)
    y = xf * lax.rsqrt(jnp.mean(xf * xf, axis=-1, keepdims=True) + EPS)
    return (y * g.astype(jnp.float32)).astype(x.dtype)


def layernorm(x, g, b):
    xf = x.astype(jnp.float32)
    mu = jnp.mean(xf, axis=-1, keepdims=True)
    xc = xf - mu
    y = xc * lax.rsqrt(jnp.mean(xc * xc, axis=-1, keepdims=True) + EPS)
    return (y * g.astype(jnp.float32) + b.astype(jnp.float32)).astype(x.dtype)


def modulate(h, shift, scale):
    return h * (1 + scale) + shift


def axial_rope_tables(s):
    rows = s // GRID_W
    row = jnp.repeat(jnp.arange(rows), GRID_W).astype(jnp.float32)
    col = jnp.tile(jnp.arange(GRID_W), rows).astype(jnp.float32)
    n_axis = HEAD_DIM // 4
    inv = ROPE_BASE ** (-jnp.arange(n_axis, dtype=jnp.float32) / n_axis)
    ang = jnp.concatenate([row[:, None] * inv, col[:, None] * inv], axis=-1)
    return jnp.cos(ang), jnp.sin(ang)


def apply_rope(x, cos, sin):
    half = HEAD_DIM // 2
    xf = x.astype(jnp.float32)
    x1, x2 = xf[..., :half], xf[..., half:]
    cs, sn = cos[None, :, None, :], sin[None, :, None, :]
    return jnp.concatenate([x1 * cs - x2 * sn, x2 * cs + x1 * sn], axis=-1).astype(x.dtype)


def split_heads(t, n):
    return t.reshape(t.shape[0], t.shape[1], n, HEAD_DIM)


def sink_softmax(sink_l, *parts):
    first = parts[0]
    sink_col = jnp.broadcast_to(sink_l.astype(jnp.float32)[None, :, :, None, None], first.shape[:-1] + (1,))
    p = jax.nn.softmax(jnp.concatenate([sink_col, *parts], axis=-1), axis=-1)
    out, off = [], 1
    for part in parts:
        out.append(p[..., off:off + part.shape[-1]])
        off += part.shape[-1]
    return out


def context_attention(qc, kc, vc, sink):
    b, n = qc.shape[:2]
    g = N_Q_HEADS // N_KV_HEADS
    qg = qc.reshape(b, n, N_KV_HEADS, g, HEAD_DIM)
    sc = jnp.einsum('bqhgd,bkhd->bhgqk', qg, kc, preferred_element_type=jnp.float32) * (HEAD_DIM ** -0.5)
    (p,) = sink_softmax(sink.reshape(N_KV_HEADS, g), sc)
    o = jnp.einsum('bhgqk,bkhd->bqhgd', p.astype(vc.dtype), vc)
    return o.reshape(b, n, ATTN_WIDTH)


def windowed_attention(q, k, v, kc, vc, sink):
    b, s = q.shape[:2]
    nb = s // BLOCK
    g = N_Q_HEADS // N_KV_HEADS
    scale = HEAD_DIM ** -0.5
    span = BLOCK + 2 * WINDOW
    qb = jnp.moveaxis(q.reshape(b, nb, BLOCK, N_KV_HEADS, g, HEAD_DIM), 1, 0)
    pad = ((0, 0), (WINDOW, WINDOW), (0, 0), (0, 0))
    kp = jnp.pad(k, pad)
    vp = jnp.pad(v, pad)
    qi = jnp.arange(BLOCK)[:, None]
    kj = jnp.arange(span)[None, :]
    band = jnp.abs((kj - WINDOW) - qi) <= WINDOW
    sink_l = sink.reshape(N_KV_HEADS, g)

    def block_fn(args):
        qblk, blk = args
        start = blk * BLOCK
        kw = lax.dynamic_slice_in_dim(kp, start, span, axis=1)
        vw = lax.dynamic_slice_in_dim(vp, start, span, axis=1)
        key_pos = start - WINDOW + jnp.arange(span)
        mask = band & ((key_pos >= 0) & (key_pos < s))[None, :]
        s_win = jnp.einsum('bqhgd,bkhd->bhgqk', qblk, kw, preferred_element_type=jnp.float32) * scale
        s_win = jnp.where(mask, s_win, -jnp.inf)
        s_ctx = jnp.einsum('bqhgd,bchd->bhgqc', qblk, kc, preferred_element_type=jnp.float32) * scale
        p_ctx, p_win = sink_softmax(sink_l, s_ctx, s_win)
        o = (jnp.einsum('bhgqc,bchd->bqhgd', p_ctx.astype(vc.dtype), vc)
             + jnp.einsum('bhgqk,bkhd->bqhgd', p_win.astype(vw.dtype), vw))
        return o

    out = lax.map(block_fn, (qb, jnp.arange(nb)))
    return jnp.moveaxis(out, 0, 1).reshape(b, s, ATTN_WIDTH)


def attention_layer(h_lat, h_ctx, w_in, sink, w_out, cos, sin, ctx_out):
    q_end, k_end, v_end = ATTN_WIDTH, ATTN_WIDTH + KV_WIDTH, ATTN_WIDTH + 2 * KV_WIDTH
    proj = h_lat @ w_in
    q = apply_rope(split_heads(proj[..., :q_end], N_Q_HEADS), cos, sin)
    k = apply_rope(split_heads(proj[..., q_end:k_end], N_KV_HEADS), cos, sin)
    v = split_heads(proj[..., k_end:v_end], N_KV_HEADS)
    z = proj[..., v_end:]
    if ctx_out:
        proj_c = h_ctx @ w_in
        kc = split_heads(proj_c[..., q_end:k_end], N_KV_HEADS)
        vc = split_heads(proj_c[..., k_end:v_end], N_KV_HEADS)
        oc = context_attention(split_heads(proj_c[..., :q_end], N_Q_HEADS), kc, vc, sink)
        y_ctx = (oc * jax.nn.silu(proj_c[..., v_end:])) @ w_out
    else:
        kv_c = h_ctx @ w_in[:, q_end:v_end]
        kc = split_heads(kv_c[..., :KV_WIDTH], N_KV_HEADS)
        vc = split_heads(kv_c[..., KV_WIDTH:], N_KV_HEADS)
        y_ctx = None
    o = windowed_attention(q, k, v, kc, vc, sink)
    y = (o * jax.nn.silu(z)) @ w_out
    return y, y_ctx


def conformer_conv_branch(h, w_in, b_in, dw_w, dw_b, ln_g, ln_b, w_out):
    proj = h @ w_in + b_in
    a, gl, z = jnp.split(proj, 3, axis=-1)
    u = a * jax.nn.sigmoid(gl)
    u = lax.conv_general_dilated(u, dw_w[:, None, :], window_strides=(1,),
                                 padding=[(CONV_K // 2, CONV_K // 2)],
                                 dimension_numbers=('NWC', 'WIO', 'NWC'),
                                 feature_group_count=CONV_WIDTH) + dw_b
    u = jax.nn.silu(layernorm(u, ln_g, ln_b))
    return (u * jax.nn.silu(z)) @ w_out


def setup_inputs(seed: int = 0) -> dict:
    key = jax.random.key(seed)
    ks = jax.random.split(key, 20)
    n_attn = (DEPTH + N_MIXERS - 1) // N_MIXERS
    n_conv = DEPTH // N_MIXERS
    nrm = jax.random.normal
    f32 = jnp.float32
    return {
        "x": nrm(ks[0], (BATCH, SEQ, D_MODEL), f32),
        "c": nrm(ks[1], (BATCH, D_MODEL), f32),
        "ctx": nrm(ks[2], (BATCH, CTX_LEN, D_MODEL), f32),
        "c_ctx": nrm(ks[3], (D_MODEL,), f32),
        "ada_w": nrm(ks[4], (DEPTH, D_MODEL, 3 * D_MODEL), f32) * D_MODEL ** -0.5,
        "ada_b": nrm(ks[5], (DEPTH, 3 * D_MODEL), f32) * 0.01,
        "norm_g": 1.0 + 0.01 * nrm(ks[6], (DEPTH, D_MODEL), f32),
        "attn_w_in": nrm(ks[7], (n_attn, D_MODEL, ATTN_PROJ), f32) * D_MODEL ** -0.5,
        "attn_sink": nrm(ks[8], (n_attn, N_Q_HEADS), f32) * 0.5,
        "attn_w_out": nrm(ks[9], (n_attn, ATTN_WIDTH, D_MODEL), f32) * ATTN_WIDTH ** -0.5,
        "conv_w_in": nrm(ks[10], (n_conv, D_MODEL, CONV_PROJ), f32) * D_MODEL ** -0.5,
        "conv_b_in": nrm(ks[11], (n_conv, CONV_PROJ), f32) * 0.01,
        "conv_dw_w": nrm(ks[12], (n_conv, CONV_K, CONV_WIDTH), f32) * CONV_K ** -0.5,
        "conv_dw_b": nrm(ks[13], (n_conv, CONV_WIDTH), f32) * 0.01,
        "conv_ln_g": 1.0 + 0.01 * nrm(ks[14], (n_conv, CONV_WIDTH), f32),
        "conv_ln_b": nrm(ks[15], (n_conv, CONV_WIDTH), f32) * 0.01,
        "conv_w_out": nrm(ks[16], (n_conv, CONV_WIDTH, D_MODEL), f32) * CONV_WIDTH ** -0.5,
        "final_g": 1.0 + 0.01 * nrm(ks[17], (D_MODEL,), f32),
    }


def reference(x, c, ctx, c_ctx, ada_w, ada_b, norm_g, attn_w_in, attn_sink, attn_w_out,
              conv_w_in, conv_b_in, conv_dw_w, conv_dw_b, conv_ln_g, conv_ln_b, conv_w_out, final_g):
    s = x.shape[1]
    cos, sin = axial_rope_tables(s)
    silu_c = jax.nn.silu(c)
    silu_cc = jax.nn.silu(c_ctx)
    h_ctx_stream = ctx
    for i in range(DEPTH):
        kind = i % N_MIXERS
        j = i // N_MIXERS
        ctx_out = any(l % N_MIXERS == 0 for l in range(i + 1, DEPTH))
        need_ctx_in = (kind == 0) or ctx_out

        shift, scale, gate = jnp.split((silu_c @ ada_w[i] + ada_b[i])[:, None, :], 3, axis=-1)
        h_lat = modulate(rmsnorm(x, norm_g[i]), shift, scale)
        h_ctx = None
        if need_ctx_in:
            shift_c, scale_c, gate_c = jnp.split(silu_cc @ ada_w[i] + ada_b[i], 3, axis=-1)
            h_ctx = modulate(rmsnorm(h_ctx_stream, norm_g[i]), shift_c, scale_c)

        if kind == 0:
            y, y_ctx = attention_layer(h_lat, h_ctx, attn_w_in[j], attn_sink[j], attn_w_out[j],
                                       cos, sin, ctx_out)
        else:
            conv_args = (conv_w_in[j], conv_b_in[j], conv_dw_w[j], conv_dw_b[j],
                         conv_ln_g[j], conv_ln_b[j], conv_w_out[j])
            y = conformer_conv_branch(h_lat, *conv_args)
            y_ctx = conformer_conv_branch(h_ctx, *conv_args) if ctx_out else None

        x = x + gate * y
        if ctx_out:
            h_ctx_stream = h_ctx_stream + gate_c * y_ctx
    return rmsnorm(x, final_g)
```

```python
import numpy as np
from contextlib import ExitStack
import concourse.bass as bass
import concourse.mybir as mybir
from concourse.bass_utils import run_bass_kernel_spmd

F32 = mybir.dt.float32
BF16 = mybir.dt.bfloat16
AF = mybir.ActivationFunctionType
ALU = mybir.AluOpType

D = 1024
S = 2048
CL = 256
NH = 16
HD = 128
EPS = 1e-6
NCORES = 8
BPC = 2
RING = 4
import os
ROPE_DEFER = os.environ.get('ROPE_DEFER', '1') == '1'
LOOKAHEAD = int(os.environ.get('LOOKAHEAD', '3'))
NDT = int(os.environ.get('NDT', '8'))
NTF = 6
NTB = 4
TT = 512

O_ADAB = 0
O_NG = O_ADAB + 96
O_FG = O_NG + 32
O_SINK = O_FG + 8
O_CB = O_SINK + 32
O_DWW = O_CB + 96
O_DWB = O_DWW + 992
O_LNG = O_DWB + 32
O_LNB = O_LNG + 32
NS = O_LNB + 32


class Prog:
    ENG = ("pe", "act", "dve", "pool", "sp")

    def __init__(self, dry):
        self.dry = dry
        self.ops = {e: [] for e in self.ENG}
        self.cnt = {}
        self.waited = {e: {} for e in self.ENG}
        self.lastw = {}
        self.readers = {}
        self.wsrc = []
        self.wpos = 0
        self.wissued = 0
        self.wlist = None
        self.rot = {}
        self.dead = set()

    def _deps(self, eng, reads, writes):
        need = {}

        def add(t):
            if t is None:
                return
            k, v = t
            if need.get(k, 0) < v:
                need[k] = v
        for r in reads:
            add(self.lastw.get(r))
            if isinstance(r, tuple) and r[0] == "ps":
                for k, v in self.readers.get(r, {}).items():
                    if k != eng:
                        add((k, v))
        for w in writes:
            add(self.lastw.get(w))
            for k, v in self.readers.get(w, {}).items():
                add((k, v))
        out = []
        wd = self.waited[eng]
        for k, v in need.items():
            if eng == "pe" and k == "pe":
                continue
            if wd.get(k, 0) >= v:
                continue
            wd[k] = v
            out.append((k, v))
        return out

    def _commit(self, tok, reads, writes):
        k, v = tok
        for r in reads:
            d = self.readers.setdefault(r, {})
            if d.get(k, 0) < v:
                d[k] = v
        for w in writes:
            self.lastw[w] = tok
            self.readers[w] = {}

    def op(self, eng, fn, reads=(), writes=()):
        if self.dry:
            return
        for r in reads:
            assert r not in self.dead, f"read of a dead (overwritten) resource {r}"
        waits = self._deps(eng, reads, writes)
        self.cnt[eng] = self.cnt.get(eng, 0) + 1
        tok = (eng, self.cnt[eng])
        self.ops[eng].append((waits, fn, eng, 1))
        self._commit(tok, reads, writes)

    def dma(self, eng, semkey, pairs, reads=(), writes=()):
        if self.dry:
            return
        waits = self._deps(eng, reads, writes)
        for i, (o, i_) in enumerate(pairs):
            self.cnt[semkey] = self.cnt.get(semkey, 0) + 16
            self.ops[eng].append((waits if i == 0 else [], (lambda e, o=o, i_=i_: e.dma_start(out=o, in_=i_)), semkey, 16))
        tok = (semkey, self.cnt[semkey])
        self._commit(tok, reads, writes)

    def barrier(self):
        if self.dry:
            return
        for e in self.ENG:
            waits = []
            for k, v in self.cnt.items():
                if k == e:
                    continue
                if self.waited[e].get(k, 0) < v:
                    self.waited[e][k] = v
                    waits.append((k, v))
            if waits:
                self.ops[e].append((waits, None, None, 0))
        self.lastw = {}
        self.readers = {}

    def rotate(self, name, n):
        i = self.rot.get(name, 0)
        self.rot[name] = (i + 1) % n
        return i


def _bc_mid(ap, n):
    a = ap.ap
    return bass.AP(ap.tensor, ap.offset, [list(a[0]), [0, n]] + [list(x) for x in a[1:]])


def _bc_last(ap, n):
    a = ap.ap
    return bass.AP(ap.tensor, ap.offset, [list(x) for x in a] + [[0, n]])


class _Stop(Exception):
    pass


def build_program(n_layers=4, final_norm=True, nb=BPC, stop=None, debug_ctx=False):
    nc = bass.Bass("TRN2", target_bir_lowering=False)
    dr = {}
    dr["xT"] = nc.dram_tensor("xT", [BPC, 8, 128, S], F32, kind="ExternalInput").ap()
    dr["cxT"] = nc.dram_tensor("cxT", [BPC, 8, 128, CL], F32, kind="ExternalInput").ap()
    dr["cT"] = nc.dram_tensor("cT", [128, 8, 3], F32, kind="ExternalInput").ap()
    dr["adaw"] = nc.dram_tensor("adaw", [4, 12, 128, 8 * 256], F32, kind="ExternalInput").ap()
    dr["small"] = nc.dram_tensor("small", [128, NS], F32, kind="ExternalInput").ap()
    dr["cb16"] = nc.dram_tensor("cb16", [128, 4 * 128], F32, kind="ExternalInput").ap()
    dr["rope"] = nc.dram_tensor("rope", [2, 128, S], F32, kind="ExternalInput").ap()
    dr["wa"] = nc.dram_tensor("wa", [2, 28, 128, 2048], F32, kind="ExternalInput").ap()
    dr["wc"] = nc.dram_tensor("wc", [2, 32, 128, 2048], F32, kind="ExternalInput").ap()
    dr["outT"] = nc.dram_tensor("outT", [BPC, 8, 128, S], F32, kind="ExternalOutput").ap()
    if stop is not None or debug_ctx:
        dr["cxo"] = nc.dram_tensor("cxo", [BPC, 8, 128, CL], F32, kind="ExternalOutput").ap()
    if stop is not None:
        dr["dH"] = nc.dram_tensor("dH", [128, 8 * 2304], BF16, kind="ExternalOutput").ap()
        dr["dLS"] = nc.dram_tensor("dLS", [128, 20864], BF16, kind="ExternalOutput").ap()
        dr["dLSF"] = nc.dram_tensor("dLSF", [128, 4096], F32, kind="ExternalOutput").ap()
        dr["dMOD"] = nc.dram_tensor("dMOD", [128, 3, 96], F32, kind="ExternalOutput").ap()

    es = ExitStack()
    with es:
        def sb(name, shape, dt):
            return es.enter_context(nc.sbuf_tensor(name, shape, dt))
        X = sb("X", [128, 8, S], F32)
        C = sb("C", [128, 8, CL], F32)
        WR = sb("WR", [128, RING, 2048], BF16)
        H = sb("H", [128, 8 * 2304], BF16)
        LSF = sb("LSF", [128, 4096], F32)
        LS = sb("LS", [128, 20864], BF16)
        TF = [sb(f"tf{i}", [128, TT], F32) for i in range(NTF)]
        TB = [sb(f"tb{i}", [128, TT], BF16) for i in range(NTB)]
        SM = sb("SM", [128, NS], F32)
        CB = sb("CB", [128, 4, 128], BF16)
        ONES = sb("ONES", [128, 128], BF16)
        MODA = sb("MODA", [128, 4, 8, 3], F32)
        MODS = sb("MODS", [128, 4, 8, 3], F32)
        MODG = sb("MODG", [128, 4, 8, 3], F32)
        MODRAW = sb("MODRAW", [128, 24, 3], F32)
        SC = sb("SC", [128, 8, 3], F32)
        SCT = sb("SCT", [128, 8, 3], F32)
        ESINK = sb("ESINK", [128, 32], F32)
        HB = sb("HB", [128, 96], F32)
        EPSD = sb("EPSD", [128, 2], F32)
        FGS = sb("FGS", [128, 8], F32)
        HLN = sb("HLN", [128, 64], F32)
        PSB = [es.enter_context(nc.psum_tensor(f"ps{i}", [128, TT], F32)) for i in range(8)]

        sem_names = list(Prog.ENG) + [f"wr{i}" for i in range(RING)] + ["ldx", "st", "cst", "cstb", "ada0", "ada1", "rope"]
        sems = {k: es.enter_context(nc.semaphore("s_" + k)) for k in sem_names}

        h_at = H[:, 0:8 * 2304].rearrange("p (k t) -> p k t", k=8)
        HCW = 1040
        h_cv = H[:, 0:8 * HCW].rearrange("p (k t) -> p k t", k=8)
        DG = H[:, 8 * HCW: 8 * HCW + 2 * 31 * 128].rearrange("p (b k j) -> p b k j", b=2, k=31)
        COS = LSF[:, 0:2048]
        SIN = LSF[:, 2048:4096]
        S1SB = LSF[:, 0:1024]
        RSTD = LSF[:, 1024:2048]
        KT = LS[:, 0:2304]
        VT = LS[:, 2304:4608].rearrange("p (b d) -> p b d", d=128)
        QT = LS[:, 4608:9216].rearrange("p (h t) -> p h t", h=2)
        SZ = LS[:, 9216:13824].rearrange("p (h t) -> p h t", h=2)
        OG = LS[:, 13824:18432].rearrange("p (h t) -> p h t", h=2)
        VC = LS[:, 0:16384].rearrange("p (c t) -> p c t", c=16)
        NG = LS[:, 16384:18432].rearrange("p (c t) -> p c t", c=2)
        UW = 1056
        UB = LS[:, 18432:18432 + 2 * UW].rearrange("p (b t) -> p b t", b=2)
        USV = LS[:, 18432 + 2 * UW: 18432 + 2 * UW + 256].rearrange("p (c t) -> p c t", c=16)

        RM = CB[:, 0, :]
        MPREV = CB[:, 1, :]
        MNEXT = CB[:, 2, :]
        IDN = CB[:, 3, :]

        def smv(off, n):
            return SM[:, off:off + n]

        def run(P):
            ATT_SCALE = float(HD) ** -0.5

            def chk(label):
                if stop is not None and label == stop:
                    raise _Stop()

            def ps_rot(lo, hi):
                return lo + P.rotate(("ps", lo, hi), hi - lo)

            def tf_next():
                i = 2 + P.rotate("tf", NTF - 2)
                return i, TF[i], ("tf", i)

            CTB = [TF[2], TF[3], TF[4], TF[5]] + [LSF[:, 2048 + 512 * i: 2048 + 512 * (i + 1)] for i in range(4)]

            def ct_next():
                i = P.rotate("ct", 8)
                return i, CTB[i], (("tf", 2 + i) if i < 4 else ("ctx", i))

            DGF = H[:, 8 * 1040: 8 * 1040 + 2 * 31 * 128].bitcast(F32)
            CT2 = CTB + [DGF[:, 512 * i: 512 * (i + 1)] for i in range(7)]
            dg_guard = set()

            def ct2_next():
                i = P.rotate("ct2", 15)
                if i < 4:
                    res = ("tf", 2 + i)
                elif i < 8:
                    res = ("ctx", i)
                else:
                    res = ("dgt", i)
                extra = []
                if i >= 8 and i not in dg_guard:
                    dg_guard.add(i)
                    extra = [("DG", 0), ("DG", 1)]
                return CT2[i], res, extra

            def rs_next():
                i = P.rotate("rs", 2)
                return i, TF[i], ("tf", i)

            def tb_next():
                i = P.rotate("tb", NTB)
                return i, TB[i], ("tb", i)

            def wslot(src):
                if P.dry:
                    P.wsrc.append(src)
                    return WR[:, 0, :], ("wr", 0)
                k = P.wpos
                P.wpos += 1
                while P.wissued < min(len(P.wlist), k + RING - 1):
                    jf = P.wissued
                    r = jf % RING
                    P.dma("pool", f"wr{r}", [(WR[:, r, :], P.wlist[jf])], reads=(), writes=[("wr", jf), ("wr", jf - RING)])
                    P.dead.add(("wr", jf - RING))
                    P.wissued += 1
                r = k % RING
                return WR[:, r, :], ("wr", k)

            def mm(out, pairs, reads, writes, flags=None):
                def fn(e):
                    n = len(pairs)
                    ins = None
                    for i, (l, r) in enumerate(pairs):
                        st, sp = (i == 0, i == n - 1) if flags is None else flags
                        ins = e.matmul(out, l, r, start=st, stop=sp, skip_group_check=True)
                    return ins
                P.op("pe", fn, reads, writes)

            def norm_mod(src, src_res, ncols, dst, dst_res, layer, row):
                n = ncols
                pb = ps_rot(0, 8)
                sqs = []
                for kc in range(8):
                    _, t, tr = tb_next()
                    if kc % 4 == 3:
                        P.op("act", (lambda e, t=t, kc=kc: e.activation(out=t[:, 0:n], in_=src[:, kc, :], func=AF.Square)),
                             reads=[src_res[kc]], writes=[tr])
                    else:
                        P.op("pool" if kc % 2 == 0 else "dve", (lambda e, t=t, kc=kc: e.tensor_tensor(out=t[:, 0:n], in0=src[:, kc, :], in1=src[:, kc, :], op=ALU.mult)),
                             reads=[src_res[kc]], writes=[tr])
                    mm(PSB[pb][:, 0:n], [(ONES[:, :], t[:, 0:n])], reads=[tr], writes=[("ps", pb)],
                       flags=(kc == 0, kc == 7))
                _, r1, r1r = tf_next()
                P.op("act", (lambda e: e.activation(out=r1[:, 0:n], in_=PSB[pb][:, 0:n], func=AF.Sqrt, bias=EPSD[:, 0:1])),
                     reads=[("ps", pb)], writes=[r1r])
                _, r2, r2r = rs_next()
                P.op("dve", (lambda e: e.reciprocal(out=r2[:, 0:n], in_=r1[:, 0:n])),
                     reads=[r1r], writes=[r2r])
                for kc in range(8):
                    _, t, tr = tf_next()
                    P.op("dve", (lambda e, t=t, kc=kc: e.tensor_tensor(out=t[:, 0:n], in0=src[:, kc, :], in1=r2[:, 0:n], op=ALU.mult)),
                         reads=[src_res[kc], r2r], writes=[tr])
                    if layer is None:
                        sc_ap = FGS[:, kc:kc + 1]
                        P.op("act", (lambda e, t=t, kc=kc, sc_ap=sc_ap: e.activation(out=dst[:, kc, :], in_=t[:, 0:n], func=AF.Identity, scale=sc_ap)),
                             reads=[tr], writes=[dst_res[kc]])
                    else:
                        sc_ap = MODA[:, layer, kc, row:row + 1]
                        bi_ap = MODS[:, layer, kc, row:row + 1]
                        P.op("act", (lambda e, t=t, kc=kc, sc_ap=sc_ap, bi_ap=bi_ap: e.activation(out=dst[:, kc, :], in_=t[:, 0:n], func=AF.Identity, scale=sc_ap, bias=bi_ap)),
                             reads=[tr], writes=[dst_res[kc]])

            XR = [[("X", kc, t) for kc in range(8)] for t in range(4)]
            CR = [("C", kc) for kc in range(8)]

            def load_x(b):
                P.dma("sp", "ldx", [(X[:, kc, :], dr["xT"][b, kc]) for kc in range(8)] + [(C[:, kc, :], dr["cxT"][b, kc]) for kc in range(8)],
                      writes=[r for t in range(4) for r in XR[t]] + CR)

            P.dma("sp", "cst", [(SM[:, :], dr["small"]), (SC[:, :, :], dr["cT"])], writes=["SM", "SC"])
            load_x(0)
            P.dma("pool", "cstb", [(CB[:, :, :].rearrange("p a b -> p (a b)"), dr["cb16"])], writes=["CB"])
            P.op("pool", lambda e: e.memset(ONES[:, :], 1.0), writes=["ONES"])
            P.op("pool", lambda e: e.memset(EPSD[:, 0:1], float(D * EPS)), writes=["EPSD"])
            P.op("pool", lambda e: e.memset(EPSD[:, 1:2], float(EPS)), reads=["EPSD"], writes=["EPSD"])
            P.op("act", lambda e: e.activation(out=SCT[:, :, :], in_=SC[:, :, :], func=AF.Tanh, scale=0.5), reads=["SC"], writes=["SCT"])
            P.op("dve", lambda e: e.scalar_tensor_tensor(out=SCT[:, :, :], in0=SCT[:, :, :], scalar=1.0, in1=SC[:, :, :], op0=ALU.add, op1=ALU.mult),
                 reads=["SC", "SCT"], writes=["SCT"])
            P.op("dve", lambda e: e.tensor_scalar(out=SCT[:, :, :], in0=SCT[:, :, :], scalar1=0.5, scalar2=None, op0=ALU.mult), reads=["SCT"], writes=["SCT"])
            P.op("act", lambda e: e.activation(out=ESINK[:, :], in_=smv(O_SINK, 32), func=AF.Exp), reads=["SM"], writes=["ESINK"])
            P.op("dve", lambda e: e.tensor_scalar(out=HB[:, :], in0=smv(O_CB, 96), scalar1=0.5, scalar2=None, op0=ALU.mult), reads=["SM"], writes=["HB"])
            P.op("dve", lambda e: e.tensor_scalar(out=FGS[:, :], in0=smv(O_FG, 8), scalar1=32.0, scalar2=None, op0=ALU.mult), reads=["SM"], writes=["FGS"])
            P.op("dve", lambda e: e.tensor_scalar(out=HLN[:, :], in0=smv(O_LNG, 64), scalar1=0.5, scalar2=None, op0=ALU.mult), reads=["SM"], writes=["HLN"])
            ADAV = [LSF[:, 0:2048].rearrange("p (k c) -> p k c", k=8), LSF[:, 2048:4096].rearrange("p (k c) -> p k c", k=8)]
            for l in range(n_layers):
                for pc in range(12):
                    bi = pc % 2
                    P.dma("sp", f"ada{bi}", [(LSF[:, bi * 2048:(bi + 1) * 2048], dr["adaw"][l, pc])], writes=[("ada", bi)])
                    for oc2 in range(2):
                        oc = pc * 2 + oc2
                        mm(PSB[0][:, oc * 3:oc * 3 + 3],
                           [(ADAV[bi][:, kc, oc2 * 128:(oc2 + 1) * 128], SCT[:, kc, :]) for kc in range(8)],
                           reads=[("ada", bi), "SCT"], writes=[("ps", 0)])
                adab = smv(O_ADAB + l * 24, 24)
                P.op("dve", (lambda e, adab=adab: e.tensor_tensor(out=MODRAW[:, :, :], in0=PSB[0][:, 0:72].rearrange("p (a b) -> p a b", b=3),
                                                                 in1=_bc_last(adab, 3), op=ALU.add)),
                     reads=[("ps", 0), "SM"], writes=["MODRAW"])
                ng = smv(O_NG + l * 8, 8)
                gsc = 0.5 if l % 2 == 0 else 0.25
                P.op("dve", (lambda e, l=l, ng=ng: e.scalar_tensor_tensor(out=MODA[:, l, :, :], in0=MODRAW[:, 8:16, :], scalar=1.0, in1=_bc_last(ng, 3), op0=ALU.add, op1=ALU.mult)),
                     reads=["MODRAW", "SM"], writes=[("MODA", l)])
                P.op("dve", (lambda e, l=l: e.tensor_scalar(out=MODA[:, l, :, :], in0=MODA[:, l, :, :], scalar1=32.0, scalar2=None, op0=ALU.mult)),
                     reads=[("MODA", l)], writes=[("MODA", l)])
                P.op("dve", (lambda e, l=l: e.tensor_copy(out=MODS[:, l, :, :], in_=MODRAW[:, 0:8, :])), reads=["MODRAW"], writes=[("MODS", l)])
                P.op("dve", (lambda e, l=l, gsc=gsc: e.tensor_scalar(out=MODG[:, l, :, :], in0=MODRAW[:, 16:24, :], scalar1=gsc, scalar2=None, op0=ALU.mult)),
                     reads=["MODRAW"], writes=[("MODG", l)])
            P.barrier()

            def attn_layer(b, l):
                j = l // 2
                ctx_out = any(k % 2 == 0 for k in range(l + 1, 4))
                P.dma("sp", "rope", [(COS, dr["rope"][0]), (SIN, dr["rope"][1])], writes=["ROPE"])
                tiles = [(0, CL)] + [(CL + TT * i, TT) for i in range(4)]
                HR = [[("H", kc, ti) for kc in range(8)] for ti in range(5)]
                for ti, (c0, n) in enumerate(tiles):
                    if ti == 0:
                        norm_mod(C[:, :, :], CR, n, h_at[:, :, c0:c0 + n], HR[ti], l, 2)
                    else:
                        norm_mod(X[:, :, (ti - 1) * TT: ti * TT], XR[ti - 1], n, h_at[:, :, c0:c0 + n], HR[ti], l, b)
                qtiles = list(range(5)) if ctx_out else list(range(1, 5))
                chk("norm")

                pend = []

                def rope_flush():
                    while pend:
                        pend.pop(0)()

                def rope_evac(pb, ti, dst, dst_res):
                    n = TT
                    t0 = (ti - 1) * TT
                    _, qb, qbr = tb_next()
                    P.op("act", lambda e: e.activation(out=qb[:, 0:n], in_=PSB[pb][:, 0:n], func=AF.Copy), reads=[("ps", pb)], writes=[qbr])

                    def rest():
                        pb2 = ps_rot(4, 8)
                        mm(PSB[pb2][:, 0:n], [(RM, qb[:, 0:n])], reads=[qbr], writes=[("ps", pb2)])
                        _, t1, t1r = tf_next()
                        P.op("dve", lambda e: e.tensor_tensor(out=t1[:, 0:n], in0=PSB[pb][:, 0:n], in1=COS[:, t0:t0 + n], op=ALU.mult),
                             reads=[("ps", pb), "ROPE"], writes=[t1r])
                        _, t2, t2r = tf_next()
                        P.op("dve", lambda e: e.tensor_tensor(out=t2[:, 0:n], in0=PSB[pb2][:, 0:n], in1=SIN[:, t0:t0 + n], op=ALU.mult),
                             reads=[("ps", pb2), "ROPE"], writes=[t2r])
                        P.op("pool", lambda e: e.tensor_tensor(out=dst, in0=t1[:, 0:n], in1=t2[:, 0:n], op=ALU.add), reads=[t1r, t2r], writes=[dst_res])
                    prev = list(pend)
                    del pend[:]
                    for f in prev:
                        f()
                    if ROPE_DEFER:
                        pend.append(rest)
                    else:
                        rest()

                for g in range(4):
                    kvs, kvr = wslot(dr["wa"][j, g * 7 + 0])
                    kv = kvs.rearrange("p (k c) -> p k c", k=8)
                    for ti, (c0, n) in enumerate(tiles):
                        pb = ps_rot(0, 4)
                        mm(PSB[pb][:, 0:n], [(kv[:, kc, 0:128], h_at[:, kc, c0:c0 + n]) for kc in range(8)],
                           reads=[kvr] + HR[ti], writes=[("ps", pb)])
                        if ti == 0:
                            P.op("act", (lambda e, pb=pb, c0=c0, n=n: e.activation(out=KT[:, c0:c0 + n], in_=PSB[pb][:, 0:n], func=AF.Copy)),
                                 reads=[("ps", pb)], writes=[("KT", ti)])
                        else:
                            rope_evac(pb, ti, KT[:, c0:c0 + n], ("KT", ti))
                        chk(f"k{ti}")
                        pb = ps_rot(0, 4)
                        nblk = n // 128
                        pairs = []

                        def vfn(e, pb=pb, c0=c0, nblk=nblk, kv=kv):
                            ins = None
                            for bk in range(nblk):
                                for kc in range(8):
                                    ins = e.matmul(PSB[pb][:, bk * 128:(bk + 1) * 128], h_at[:, kc, c0 + bk * 128: c0 + (bk + 1) * 128], kv[:, kc, 128:256],
                                                   start=(kc == 0), stop=(kc == 7), skip_group_check=True)
                            return ins
                        P.op("pe", vfn, reads=[kvr] + HR[ti], writes=[("ps", pb)])
                        b0 = c0 // 128
                        P.op("act", (lambda e, pb=pb, b0=b0, nblk=nblk, n=n: e.activation(out=VT[:, b0:b0 + nblk, :], in_=PSB[pb][:, 0:n].rearrange("p (b d) -> p b d", d=128), func=AF.Copy)),
                             reads=[("ps", pb)], writes=[("VT", ti)])
                        chk(f"v{ti}")
                    rope_flush()
                    chk("kv")
                    for pr in range(2):
                        qs, qr = wslot(dr["wa"][j, g * 7 + 1 + pr * 3])
                        qv = qs.rearrange("p (k c) -> p k c", k=8)
                        for hh in range(2):
                            for ti in qtiles:
                                c0, n = tiles[ti]
                                pb = ps_rot(0, 4)
                                mm(PSB[pb][:, 0:n], [(qv[:, kc, hh * 128:(hh + 1) * 128], h_at[:, kc, c0:c0 + n]) for kc in range(8)],
                                   reads=[qr] + HR[ti], writes=[("ps", pb)])
                                if ti == 0:
                                    P.op("act", (lambda e, pb=pb, c0=c0, n=n, hh=hh: e.activation(out=QT[:, hh, c0:c0 + n], in_=PSB[pb][:, 0:n], func=AF.Copy)),
                                         reads=[("ps", pb)], writes=[("QT", hh, ti)])
                                else:
                                    rope_evac(pb, ti, QT[:, hh, c0:c0 + n], ("QT", hh, ti))
                        rope_flush()
                        zs, zr = wslot(dr["wa"][j, g * 7 + 2 + pr * 3])
                        zv = zs.rearrange("p (k c) -> p k c", k=8)
                        for hh in range(2):
                            for ti in qtiles:
                                c0, n = tiles[ti]
                                pb = ps_rot(0, 4)
                                mm(PSB[pb][:, 0:n], [(zv[:, kc, hh * 128:(hh + 1) * 128], h_at[:, kc, c0:c0 + n]) for kc in range(8)],
                                   reads=[zr] + HR[ti], writes=[("ps", pb)])
                                _, t, tr = tf_next()
                                P.op("act", (lambda e, pb=pb, n=n, t=t: e.activation(out=t[:, 0:n], in_=PSB[pb][:, 0:n], func=AF.Tanh, scale=0.5)),
                                     reads=[("ps", pb)], writes=[tr])
                                P.op("dve", (lambda e, pb=pb, n=n, t=t, hh=hh, c0=c0: e.scalar_tensor_tensor(out=SZ[:, hh, c0:c0 + n], in0=t[:, 0:n], scalar=1.0, in1=PSB[pb][:, 0:n], op0=ALU.add, op1=ALU.mult)),
                                     reads=[("ps", pb), tr], writes=[("SZ", hh, ti)])
                        chk("qz")
                        for hh in range(2):
                            head = 4 * g + 2 * pr + hh
                            for ti in qtiles:
                                c0, n = tiles[ti]
                                otb = 4 + P.rotate("ot", 2)
                                lb = 6 + P.rotate("lb", 2)
                                kblocks = [(0, 0, n, None, None), (1, 0, n, None, None)]
                                if ti > 0:
                                    qb0 = 4 * (ti - 1)
                                    for kb in range(qb0 - 1, qb0 + 5):
                                        if kb < 0 or kb >= 16:
                                            continue
                                        qlo = max(qb0, kb - 1)
                                        qhi = min(qb0 + 3, kb + 1)
                                        kblocks.append((2 + kb, (qlo - qb0) * 128, (qhi - qlo + 1) * 128, kb, qb0))
                                nk = len(kblocks)
                                pts = {}

                                def emit_s(bi_):
                                    vb, cc0, cn, kb, qb0 = kblocks[bi_]
                                    sbk = ps_rot(0, 4)
                                    kti = 0 if vb < 2 else 1 + (vb - 2) // 4
                                    masks = []
                                    if kb is not None:
                                        for qb in range(qb0 + cc0 // 128, qb0 + (cc0 + cn) // 128):
                                            m = MPREV if qb == kb + 1 else (MNEXT if qb == kb - 1 else None)
                                            if m is not None:
                                                masks.append(((qb - qb0) * 128, m))

                                    def sfn(e, sbk=sbk, vb=vb, cc0=cc0, cn=cn, masks=masks, hh=hh, c0=c0):
                                        nm = len(masks)
                                        ins = e.matmul(PSB[sbk][:, cc0:cc0 + cn], KT[:, vb * 128:(vb + 1) * 128], QT[:, hh, c0 + cc0:c0 + cc0 + cn],
                                                       start=True, stop=(nm == 0), skip_group_check=True)
                                        for mi, (o_, m) in enumerate(masks):
                                            ins = e.matmul(PSB[sbk][:, o_:o_ + 128], IDN, m, start=False, stop=(mi == nm - 1), skip_group_check=True)
                                        return ins
                                    P.op("pe", sfn, reads=[("KT", kti), ("QT", hh, ti)], writes=[("ps", sbk)])
                                    _, pt, ptr = tb_next()
                                    pts[bi_] = (pt, ptr)
                                    P.op("act", (lambda e, sbk=sbk, cc0=cc0, cn=cn, pt=pt: e.activation(out=pt[:, cc0:cc0 + cn], in_=PSB[sbk][:, cc0:cc0 + cn], func=AF.Exp, scale=ATT_SCALE)),
                                         reads=[("ps", sbk)], writes=[ptr])

                                def emit_pv(bi_):
                                    vb, cc0, cn, kb, qb0 = kblocks[bi_]
                                    kti = 0 if vb < 2 else 1 + (vb - 2) // 4
                                    pt, ptr = pts.pop(bi_)
                                    mm(PSB[otb][:, cc0:cc0 + cn], [(VT[:, vb, :], pt[:, cc0:cc0 + cn])], reads=[("VT", kti), ptr], writes=[("ps", otb)],
                                       flags=(bi_ == 0, bi_ == nk - 1))
                                    mm(PSB[lb][:, cc0:cc0 + cn], [(ONES[:, :], pt[:, cc0:cc0 + cn])], reads=[ptr], writes=[("ps", lb)],
                                       flags=(bi_ == 0, bi_ == nk - 1))

                                LOOK = LOOKAHEAD
                                for bi_ in range(min(LOOK, nk)):
                                    emit_s(bi_)
                                for bi_ in range(nk):
                                    emit_pv(bi_)
                                    if bi_ + LOOK < nk:
                                        emit_s(bi_ + LOOK)
                                _, d1, d1r = tf_next()
                                es_ap = ESINK[:, j * 16 + head: j * 16 + head + 1]
                                P.op("act", (lambda e, lb=lb, n=n, d1=d1, es_ap=es_ap: e.activation(out=d1[:, 0:n], in_=PSB[lb][:, 0:n], func=AF.Identity, bias=es_ap)),
                                     reads=[("ps", lb)], writes=[d1r])
                                _, d2, d2r = tf_next()
                                P.op("dve", (lambda e, n=n, d1=d1, d2=d2: e.reciprocal(out=d2[:, 0:n], in_=d1[:, 0:n])), reads=[d1r], writes=[d2r])
                                _, d3, d3r = tf_next()
                                P.op("pool", (lambda e, n=n, d2=d2, d3=d3, hh=hh, c0=c0: e.tensor_tensor(out=d3[:, 0:n], in0=d2[:, 0:n], in1=SZ[:, hh, c0:c0 + n], op=ALU.mult)),
                                     reads=[d2r, ("SZ", hh, ti)], writes=[d3r])
                                P.op("dve", (lambda e, n=n, d3=d3, hh=hh, c0=c0, otb=otb: e.tensor_tensor(out=OG[:, hh, c0:c0 + n], in0=PSB[otb][:, 0:n], in1=d3[:, 0:n], op=ALU.mult)),
                                     reads=[("ps", otb), d3r], writes=[("OG", hh, ti)])
                        ws_, wr_ = wslot(dr["wa"][j, g * 7 + 3 + pr * 3])
                        wv = ws_.rearrange("p (k c) -> p k c", k=2)
                        for ti in qtiles:
                            def emit_wout(ti=ti, wv=wv, wr_=wr_):
                                c0, n = tiles[ti]
                                for mc in range(8):
                                    pb = ps_rot(0, 4)
                                    mm(PSB[pb][:, 0:n], [(wv[:, h2, mc * 128:(mc + 1) * 128], OG[:, h2, c0:c0 + n]) for h2 in range(2)],
                                       reads=[wr_, ("OG", 0, ti), ("OG", 1, ti)], writes=[("ps", pb)])
                                    if ti == 0:
                                        dst = C[:, mc, :]
                                        dres = CR[mc]
                                        gt = MODG[:, l, mc, 2:3]
                                    else:
                                        dst = X[:, mc, (ti - 1) * TT: ti * TT]
                                        dres = XR[ti - 1][mc]
                                        gt = MODG[:, l, mc, b:b + 1]
                                    if mc % 4 == 2:
                                        _, tu, tur = tf_next()
                                        P.op("act", (lambda e, pb=pb, n=n, tu=tu, gt=gt: e.activation(out=tu[:, 0:n], in_=PSB[pb][:, 0:n], func=AF.Identity, scale=gt)),
                                             reads=[("ps", pb)], writes=[tur])
                                        P.op("pool", (lambda e, n=n, tu=tu, dst=dst: e.tensor_tensor(out=dst, in0=dst, in1=tu[:, 0:n], op=ALU.add)),
                                             reads=[tur, dres], writes=[dres])
                                    else:
                                        P.op("dve", (lambda e, pb=pb, n=n, dst=dst, gt=gt: e.scalar_tensor_tensor(out=dst, in0=PSB[pb][:, 0:n], scalar=gt, in1=dst, op0=ALU.mult, op1=ALU.add)),
                                             reads=[("ps", pb), dres], writes=[dres])
                            emit_wout()
                P.barrier()

            def conv_layer(b, l):
                j = l // 2
                ctx_out = any(k % 2 == 0 for k in range(l + 1, 4))
                ranges = ([("ctx", 0, CL)] if ctx_out else []) + [("lat", 0, 1024), ("lat", 1024, 1024)]
                for (kind, r0, rl) in ranges:
                    row = 2 if kind == "ctx" else b
                    halo_r = (kind == "lat" and r0 == 0)
                    halo_l = (kind == "lat" and r0 > 0)
                    rt = []
                    if kind == "ctx":
                        rt.append((0, CL, C[:, :, :], CR))
                    else:
                        for i in range(2):
                            t = r0 // TT + i
                            rt.append((i * TT, TT, X[:, :, t * TT:(t + 1) * TT], XR[t]))
                        if halo_r:
                            rt.append((1024, 15, X[:, :, 1024:1039], XR[2]))
                    HR = [[("H", kc, ti) for kc in range(8)] for ti in range(len(rt))]
                    for ti, (c0, n, sv, sr) in enumerate(rt):
                        norm_mod(sv, sr, n, h_cv[:, :, c0:c0 + n], HR[ti], l, row)
                    outt = [x for x in rt if x[1] > 15]
                    nst = len(outt)
                    def p1_front(cc):
                            ws_, wr_ = wslot(dr["wc"][j, cc])
                            wv = ws_.rearrange("p (k c) -> p k c", k=8)
                            ub = P.rotate("ub", 2)
                            U = UB[:, ub, :]
                            ures = ("U", ub)
                            dww = smv(O_DWW + (j * 16 + cc) * 31, 31)
                            P.op("dve", (lambda e, ub=ub, dww=dww: e.tensor_tensor(out=DG[:, ub, :, :], in0=_bc_mid(IDN, 31), in1=_bc_last(dww, 128), op=ALU.mult)),
                                 reads=[], writes=[("DG", ub)])
                            if halo_l:
                                P.op("pool", (lambda e, U=U, cc=cc: e.tensor_copy(out=U[:, 0:15], in_=USV[:, cc, 0:15])), reads=[("USV", cc)], writes=[ures])
                            else:
                                P.op("pool", (lambda e, U=U: e.memset(U[:, 0:15], 0.0)), writes=[ures])
                            if not halo_r:
                                P.op("pool", (lambda e, U=U, rl=rl: e.memset(U[:, 15 + rl:15 + rl + 15], 0.0)), reads=[ures], writes=[ures])
                            for ti, (c0, n, sv, sr) in enumerate(rt):
                                pa = ps_rot(0, 4)
                                mm(PSB[pa][:, 0:n], [(wv[:, kc, 0:128], h_cv[:, kc, c0:c0 + n]) for kc in range(8)], reads=[wr_] + HR[ti], writes=[("ps", pa)])
                                pg = ps_rot(0, 4)
                                mm(PSB[pg][:, 0:n], [(wv[:, kc, 128:256], h_cv[:, kc, c0:c0 + n]) for kc in range(8)], reads=[wr_] + HR[ti], writes=[("ps", pg)])
                                _, t1, t1r = tf_next()
                                hbg = HB[:, j * 48 + 16 + cc: j * 48 + 16 + cc + 1]
                                hba = HB[:, j * 48 + cc: j * 48 + cc + 1]
                                P.op("act", (lambda e, pg=pg, n=n, t1=t1, hbg=hbg: e.activation(out=t1[:, 0:n], in_=PSB[pg][:, 0:n], func=AF.Tanh, scale=0.5, bias=hbg)),
                                     reads=[("ps", pg)], writes=[t1r])
                                _, t2, t2r = tf_next()
                                P.op("act", (lambda e, pa=pa, n=n, t2=t2, hba=hba: e.activation(out=t2[:, 0:n], in_=PSB[pa][:, 0:n], func=AF.Identity, scale=0.5, bias=hba)),
                                     reads=[("ps", pa)], writes=[t2r])
                                P.op("dve", (lambda e, n=n, t1=t1, t2=t2, U=U, c0=c0: e.scalar_tensor_tensor(out=U[:, 15 + c0:15 + c0 + n], in0=t1[:, 0:n], scalar=1.0, in1=t2[:, 0:n], op0=ALU.add, op1=ALU.mult)),
                                     reads=[t1r, t2r, ures], writes=[ures])
                            if halo_r:
                                P.op("pool", (lambda e, U=U, cc=cc: e.tensor_copy(out=USV[:, cc, 0:15], in_=U[:, 15 + 1009:15 + 1024])), reads=[ures], writes=[("USV", cc)])
                            return ub, U, ures

                    def p1_back(cc, ub, U, ures):
                            dwb = smv(O_DWB + j * 16 + cc, 1)
                            accs = []
                            for si, (c0, n, sv, sr) in enumerate(outt):
                                if NDT > 0:
                                    _, ac, acr = ct_next()
                                else:
                                    ac, acr = None, None
                                accs.append((ac, acr))
                            for k in range(NDT):
                                wk = smv(O_DWW + (j * 16 + cc) * 31 + k, 1)
                                for si, (c0, n, sv, sr) in enumerate(outt):
                                    ac, acr = accs[si]
                                    if k == 0:
                                        P.op("dve", (lambda e, ac=ac, n=n, U=U, c0=c0, k=k, wk=wk: e.tensor_scalar(out=ac[:, 0:n], in0=U[:, c0 + k:c0 + k + n], scalar1=wk, scalar2=None, op0=ALU.mult)),
                                             reads=[ures], writes=[acr])
                                    else:
                                        P.op("dve", (lambda e, ac=ac, n=n, U=U, c0=c0, k=k, wk=wk: e.scalar_tensor_tensor(out=ac[:, 0:n], in0=U[:, c0 + k:c0 + k + n], scalar=wk, in1=ac[:, 0:n], op0=ALU.mult, op1=ALU.add)),
                                             reads=[ures, acr], writes=[acr])
                            for si, (c0, n, sv, sr) in enumerate(outt):
                                ac, acr = accs[si]
                                pv = ps_rot(0, 4)
                                mm(PSB[pv][:, 0:n], [(DG[:, ub, k, :], U[:, c0 + k:c0 + k + n]) for k in range(NDT, 31)], reads=[("DG", ub), ures], writes=[("ps", pv)])
                                if NDT > 0:
                                    P.op("dve", (lambda e, pv=pv, n=n, ac=ac: e.tensor_tensor(out=ac[:, 0:n], in0=PSB[pv][:, 0:n], in1=ac[:, 0:n], op=ALU.add)),
                                         reads=[("ps", pv), acr], writes=[acr])
                                    vsrc = ac
                                    vres = acr
                                else:
                                    vsrc = PSB[pv]
                                    vres = ("ps", pv)
                                P.op("act", (lambda e, vsrc=vsrc, n=n, cc=cc, c0=c0, dwb=dwb: e.activation(out=VC[:, cc, c0:c0 + n], in_=vsrc[:, 0:n], func=AF.Identity, bias=dwb)),
                                     reads=[vres], writes=[("VC", cc, si)])
                                _, sq, sqr = tb_next()
                                P.op("act", (lambda e, vsrc=vsrc, n=n, sq=sq, dwb=dwb: e.activation(out=sq[:, 0:n], in_=vsrc[:, 0:n], func=AF.Square, bias=dwb)),
                                     reads=[vres], writes=[sqr])
                                mm(PSB[4 + si][:, 0:n], [(ONES[:, :], VC[:, cc, c0:c0 + n])], reads=[("VC", cc, si)], writes=[("ps", 4 + si)], flags=(cc == 0, cc == 15))
                                mm(PSB[6 + si][:, 0:n], [(ONES[:, :], sq[:, 0:n])], reads=[sqr], writes=[("ps", 6 + si)], flags=(cc == 0, cc == 15))

                    fctx = p1_front(0)
                    for cc in range(16):
                        nxt = p1_front(cc + 1) if cc + 1 < 16 else None
                        p1_back(cc, *fctx)
                        fctx = nxt
                    for si, (c0, n, sv, sr) in enumerate(outt):
                        P.op("dve", (lambda e, si=si, c0=c0, n=n: e.tensor_scalar(out=S1SB[:, c0:c0 + n], in0=PSB[4 + si][:, 0:n], scalar1=1.0 / 2048, scalar2=None, op0=ALU.mult)),
                             reads=[("ps", 4 + si)], writes=[("S1SB", si)])
                        _, t1, t1r = tf_next()
                        P.op("dve", (lambda e, c0=c0, n=n, t1=t1: e.tensor_tensor(out=t1[:, 0:n], in0=S1SB[:, c0:c0 + n], in1=S1SB[:, c0:c0 + n], op=ALU.mult)),
                             reads=[("S1SB", si)], writes=[t1r])
                        _, t2, t2r = tf_next()
                        P.op("dve", (lambda e, si=si, n=n, t1=t1, t2=t2: e.scalar_tensor_tensor(out=t2[:, 0:n], in0=PSB[6 + si][:, 0:n], scalar=1.0 / 2048, in1=t1[:, 0:n], op0=ALU.mult, op1=ALU.subtract)),
                             reads=[("ps", 6 + si), t1r], writes=[t2r])
                        _, t3, t3r = tf_next()
                        P.op("act", (lambda e, n=n, t2=t2, t3=t3: e.activation(out=t3[:, 0:n], in_=t2[:, 0:n], func=AF.Sqrt, bias=EPSD[:, 1:2])),
                             reads=[t2r], writes=[t3r])
                        P.op("dve", (lambda e, c0=c0, n=n, t3=t3: e.reciprocal(out=RSTD[:, c0:c0 + n], in_=t3[:, 0:n])),
                             reads=[t3r], writes=[("RSTD", si)])
                    cpend = []
                    s2pend = []
                    dg_guard.clear()
                    for pz in range(8):
                        zs, zr = wslot(dr["wc"][j, 16 + pz * 2])
                        zv = zs.rearrange("p (k c) -> p k c", k=8)
                        woh = {}
                        for si, (c0, n, sv, sr) in enumerate(outt):
                            keep = []
                            for tag, f in cpend:
                                if tag == si:
                                    f()
                                else:
                                    keep.append((tag, f))
                            cpend[:] = keep
                            for ci in range(2):
                                cc = pz * 2 + ci
                                bz = smv(O_CB + j * 48 + 32 + cc, 1)
                                lng = smv(O_LNG + j * 16 + cc, 1)
                                lnb = smv(O_LNB + j * 16 + cc, 1)
                                hbz = HB[:, j * 48 + 32 + cc: j * 48 + 32 + cc + 1]
                                hlg = HLN[:, j * 16 + cc: j * 16 + cc + 1]
                                hlb = HLN[:, 32 + j * 16 + cc: 32 + j * 16 + cc + 1]
                                tB, tBr, xB = ct2_next()
                                tC, tCr, xC = ct2_next()
                                P.op("dve", (lambda e, n=n, tB=tB, cc=cc, c0=c0: e.tensor_tensor(out=tB[:, 0:n], in0=VC[:, cc, c0:c0 + n], in1=S1SB[:, c0:c0 + n], op=ALU.subtract)),
                                     reads=[("VC", cc, si), ("S1SB", si)], writes=[tBr] + xB)
                                P.op("pool", (lambda e, n=n, tB=tB, c0=c0: e.tensor_tensor(out=tB[:, 0:n], in0=tB[:, 0:n], in1=RSTD[:, c0:c0 + n], op=ALU.mult)),
                                     reads=[tBr, ("RSTD", si)], writes=[tBr])
                                P.op("act", (lambda e, n=n, tB=tB, tC=tC, hlg=hlg, hlb=hlb: e.activation(out=tC[:, 0:n], in_=tB[:, 0:n], func=AF.Tanh, scale=hlg, bias=hlb)),
                                     reads=[tBr], writes=[tCr] + xC)
                                P.op("act", (lambda e, n=n, tB=tB, lng=lng, lnb=lnb: e.activation(out=tB[:, 0:n], in_=tB[:, 0:n], func=AF.Identity, scale=lng, bias=lnb)),
                                     reads=[tBr], writes=[tBr])

                                def stage2(n=n, c0=c0, si=si, ci=ci, cc=cc, tB=tB, tBr=tBr, tC=tC, tCr=tCr, bz=bz, hbz=hbz, zv=zv, zr=zr, HRs=HR[si]):
                                    tA, tAr, xA = ct2_next()
                                    tD, tDr, xD = ct2_next()
                                    P.op("dve", (lambda e: e.scalar_tensor_tensor(out=tC[:, 0:n], in0=tC[:, 0:n], scalar=1.0, in1=tB[:, 0:n], op0=ALU.add, op1=ALU.mult)),
                                         reads=[tBr, tCr], writes=[tCr])
                                    pzb = ps_rot(0, 8)
                                    mm(PSB[pzb][:, 0:n], [(zv[:, kc, ci * 128:(ci + 1) * 128], h_cv[:, kc, c0:c0 + n]) for kc in range(8)], reads=[zr] + HRs, writes=[("ps", pzb)])
                                    P.op("act", (lambda e: e.activation(out=tA[:, 0:n], in_=PSB[pzb][:, 0:n], func=AF.Identity, bias=bz)),
                                         reads=[("ps", pzb)], writes=[tAr] + xA)
                                    P.op("act", (lambda e: e.activation(out=tD[:, 0:n], in_=PSB[pzb][:, 0:n], func=AF.Tanh, scale=0.5, bias=hbz)),
                                         reads=[("ps", pzb)], writes=[tDr] + xD)
                                    P.op("dve", (lambda e: e.scalar_tensor_tensor(out=tA[:, 0:n], in0=tD[:, 0:n], scalar=1.0, in1=tA[:, 0:n], op0=ALU.add, op1=ALU.mult)),
                                         reads=[tAr, tDr], writes=[tAr])
                                    P.op("pool", (lambda e: e.tensor_tensor(out=NG[:, ci, c0:c0 + n], in0=tC[:, 0:n], in1=tA[:, 0:n], op=ALU.mult)),
                                         reads=[tAr, tCr], writes=[("NG", ci, si)])
                                    if cpend:
                                        cpend.pop(0)[1]()
                                for f in s2pend:
                                    f()
                                del s2pend[:]
                                s2pend.append(stage2)
                            for f in s2pend:
                                f()
                            del s2pend[:]
                            def emit_cwout(mcs, si=si, c0=c0, n=n, sv=sv, sr=sr, woh=woh, pz=pz):
                                if "w" not in woh:
                                    woh["w"] = wslot(dr["wc"][j, 16 + pz * 2 + 1])
                                ws_, wr_ = woh["w"]
                                wv = ws_.rearrange("p (k c) -> p k c", k=2)
                                for mc in mcs:
                                    pb = ps_rot(0, 8)
                                    mm(PSB[pb][:, 0:n], [(wv[:, ci2, mc * 128:(mc + 1) * 128], NG[:, ci2, c0:c0 + n]) for ci2 in range(2)],
                                       reads=[wr_, ("NG", 0, si), ("NG", 1, si)], writes=[("ps", pb)])
                                    dst = sv[:, mc, :]
                                    gt = MODG[:, l, mc, row:row + 1]
                                    if mc % 4 == 2:
                                        tu, tur, xu = ct2_next()
                                        P.op("act", (lambda e, pb=pb, n=n, tu=tu, gt=gt: e.activation(out=tu[:, 0:n], in_=PSB[pb][:, 0:n], func=AF.Identity, scale=gt)),
                                             reads=[("ps", pb)], writes=[tur] + xu)
                                        P.op("pool", (lambda e, n=n, tu=tu, dst=dst: e.tensor_tensor(out=dst, in0=dst, in1=tu[:, 0:n], op=ALU.add)),
                                             reads=[tur, sr[mc]], writes=[sr[mc]])
                                    else:
                                        P.op("dve", (lambda e, pb=pb, n=n, dst=dst, gt=gt: e.scalar_tensor_tensor(out=dst, in0=PSB[pb][:, 0:n], scalar=gt, in1=dst, op0=ALU.mult, op1=ALU.add)),
                                             reads=[("ps", pb), sr[mc]], writes=[sr[mc]])
                            cpend.append((si, lambda f=emit_cwout: f(range(0, 4))))
                            cpend.append((si, lambda f=emit_cwout: f(range(4, 8))))
                    for tag, f in cpend:
                        f()
                    del cpend[:]
                    P.barrier()

            for b in range(nb):
                if b > 0:
                    load_x(b)
                try:
                    for l in range(n_layers):
                        if l % 2 == 0:
                            attn_layer(b, l)
                        else:
                            conv_layer(b, l)
                except _Stop:
                    P.barrier()
                    P.dma("sp", "st", [(dr["dH"], H[:, :]), (dr["dLS"], LS[:, :]), (dr["dLSF"], LSF[:, :]),
                                       (dr["dMOD"][:, 0, :], MODA[:, :, :, :].rearrange("p a b c -> p (a b c)")),
                                       (dr["dMOD"][:, 1, :], MODS[:, :, :, :].rearrange("p a b c -> p (a b c)")),
                                       (dr["dMOD"][:, 2, :], MODG[:, :, :, :].rearrange("p a b c -> p (a b c)"))])
                if final_norm:
                    for t in range(4):
                        xs = X[:, :, t * TT:(t + 1) * TT]
                        norm_mod(xs, XR[t], TT, xs, XR[t], None, None)
                P.dma("sp", "st", [(dr["outT"][b, kc], X[:, kc, :]) for kc in range(8)]
                      + ([(dr["cxo"][b, kc], C[:, kc, :]) for kc in range(8)] if "cxo" in dr else []),
                      reads=[r for t in range(4) for r in XR[t]] + CR)
            P.barrier()

        Pd = Prog(dry=True)
        run(Pd)
        P = Prog(dry=False)
        P.wlist = Pd.wsrc
        run(P)
        global _LAST_PROG
        _LAST_PROG = P

        with nc.Block() as block:
            def replay(eng, e):
                for waits, fn, semkey, inc in P.ops[eng]:
                    for k, v in waits:
                        e.wait_ge(sems[k], v)
                    if fn is not None:
                        ins = fn(e)
                        ins.then_inc(sems[semkey], inc)

            @block.tensor
            def _(e):
                replay("pe", e)

            @block.scalar
            def _(e):
                replay("act", e)

            @block.vector
            def _(e):
                replay("dve", e)

            @block.gpsimd
            def _(e):
                replay("pool", e)

            @block.sync
            def _(e):
                replay("sp", e)
    return nc


def _rope_tables():
    rows = S // 64
    row = np.repeat(np.arange(rows), 64).astype(np.float32)
    col = np.tile(np.arange(64), rows).astype(np.float32)
    n_axis = HD // 4
    inv = (10000.0 ** (-np.arange(n_axis, dtype=np.float32) / n_axis)).astype(np.float32)
    ang = np.concatenate([row[:, None] * inv, col[:, None] * inv], axis=-1).astype(np.float32)
    cos = np.cos(ang).astype(np.float32).T
    sin = np.sin(ang).astype(np.float32).T
    return np.ascontiguousarray(np.stack([np.concatenate([cos, cos], 0), np.concatenate([sin, sin], 0)], 0))


def _const16():
    rm = np.zeros((128, 128), np.float32)
    for m in range(64):
        rm[m + 64, m] = -1.0
        rm[m, m + 64] = 1.0
    b = np.arange(128)[:, None]
    a = np.arange(128)[None, :]
    mprev = np.where(b >= a, 0.0, -30000.0).astype(np.float32)
    mnext = np.where(b <= a, 0.0, -30000.0).astype(np.float32)
    idn = np.eye(128, dtype=np.float32)
    return np.ascontiguousarray(np.stack([rm, mprev, mnext, idn], 1).reshape(128, 512))


def _pk(v):
    return np.ascontiguousarray(v.reshape(-1, 128).T)


def _wslot_k8(w, cols):
    s = w[:, cols].reshape(8, 128, len(cols)).transpose(1, 0, 2)
    return s.reshape(128, -1)


def _wslot_out(w, r0):
    s = w[r0:r0 + 256].reshape(2, 128, 1024).transpose(1, 0, 2)
    return s.reshape(128, -1)


def _prep_shared(inp):
    f = np.float32
    ada_w = np.asarray(inp["ada_w"], f)
    adaw = np.ascontiguousarray(ada_w.reshape(4, 8, 128, 12, 256).transpose(0, 3, 2, 1, 4).reshape(4, 12, 128, 2048))
    small = np.zeros((128, NS), f)
    for l in range(4):
        small[:, O_ADAB + l * 24:O_ADAB + (l + 1) * 24] = _pk(np.asarray(inp["ada_b"], f)[l])
        small[:, O_NG + l * 8:O_NG + (l + 1) * 8] = _pk(np.asarray(inp["norm_g"], f)[l])
    small[:, O_FG:O_FG + 8] = _pk(np.asarray(inp["final_g"], f))
    small[:, O_SINK:O_SINK + 32] = np.asarray(inp["attn_sink"], f).reshape(1, 32)
    for j in range(2):
        small[:, O_CB + j * 48:O_CB + (j + 1) * 48] = _pk(np.asarray(inp["conv_b_in"], f)[j])
        dw = np.asarray(inp["conv_dw_w"], f)[j]
        small[:, O_DWW + j * 496:O_DWW + (j + 1) * 496] = dw.T.reshape(16, 128, 31).transpose(1, 0, 2).reshape(128, 496)
        small[:, O_DWB + j * 16:O_DWB + (j + 1) * 16] = _pk(np.asarray(inp["conv_dw_b"], f)[j])
        small[:, O_LNG + j * 16:O_LNG + (j + 1) * 16] = _pk(np.asarray(inp["conv_ln_g"], f)[j])
        small[:, O_LNB + j * 16:O_LNB + (j + 1) * 16] = _pk(np.asarray(inp["conv_ln_b"], f)[j])
    wa = np.zeros((2, 28, 128, 2048), f)
    awi = np.asarray(inp["attn_w_in"], f)
    awo = np.asarray(inp["attn_w_out"], f)
    for j in range(2):
        for g in range(4):
            kvc = list(range(2048 + g * 128, 2048 + (g + 1) * 128)) + list(range(2560 + g * 128, 2560 + (g + 1) * 128))
            wa[j, g * 7 + 0] = _wslot_k8(awi[j], kvc)
            for pr in range(2):
                q0 = g * 512 + pr * 256
                wa[j, g * 7 + 1 + pr * 3] = _wslot_k8(awi[j], list(range(q0, q0 + 256)))
                wa[j, g * 7 + 2 + pr * 3] = _wslot_k8(awi[j], list(range(3072 + q0, 3072 + q0 + 256)))
                wa[j, g * 7 + 3 + pr * 3] = _wslot_out(awo[j], q0)
    wc = np.zeros((2, 32, 128, 2048), f)
    cwi = np.asarray(inp["conv_w_in"], f)
    cwo = np.asarray(inp["conv_w_out"], f)
    for j in range(2):
        for cc in range(16):
            cols = list(range(cc * 128, (cc + 1) * 128)) + list(range(2048 + cc * 128, 2048 + (cc + 1) * 128))
            wc[j, cc] = _wslot_k8(cwi[j], cols)
        for pz in range(8):
            wc[j, 16 + pz * 2] = _wslot_k8(cwi[j], list(range(4096 + pz * 256, 4096 + (pz + 1) * 256)))
            wc[j, 16 + pz * 2 + 1] = _wslot_out(cwo[j], pz * 256)
    return {"adaw": adaw, "small": small, "cb16": _const16(), "rope": _rope_tables(), "wa": wa, "wc": wc}


def _prep_core(inp, shared, core):
    f = np.float32
    b0 = core * BPC
    x = np.asarray(inp["x"], f)[b0:b0 + BPC]
    ctx = np.asarray(inp["ctx"], f)[b0:b0 + BPC]
    xT = np.ascontiguousarray(x.transpose(0, 2, 1).reshape(BPC, 8, 128, S))
    cxT = np.ascontiguousarray(ctx.transpose(0, 2, 1).reshape(BPC, 8, 128, CL))
    c = np.asarray(inp["c"], f)[b0:b0 + BPC]
    rows = np.concatenate([c, np.asarray(inp["c_ctx"], f)[None, :]], 0)
    cT = np.ascontiguousarray(rows.reshape(3, 8, 128).transpose(2, 1, 0))
    m = {"xT": xT, "cxT": cxT, "cT": cT}
    m.update(shared)
    return m


_NC_CACHE = {}


def kernel(**inputs):
    if "nc" not in _NC_CACHE:
        _NC_CACHE["nc"] = build_program()
    nc = _NC_CACHE["nc"]
    shared = _prep_shared(inputs)
    in_maps = [_prep_core(inputs, shared, c) for c in range(NCORES)]
    res = run_bass_kernel_spmd(nc, in_maps, core_ids=list(range(NCORES)))
    outs = []
    for c in range(NCORES):
        o = np.asarray(res.results[c]["outT"], np.float32).reshape(BPC, D, S)
        outs.append(o.transpose(0, 2, 1))
    return np.ascontiguousarray(np.concatenate(outs, 0)).astype(np.float32)
```

```python
import numpy as np
from contextlib import ExitStack
import concourse.bass as bass
import concourse.mybir as mybir
from concourse.bass_utils import run_bass_kernel_spmd

F32 = mybir.dt.float32
BF16 = mybir.dt.bfloat16
AF = mybir.ActivationFunctionType
ALU = mybir.AluOpType

D = 1024
S = 2048
CL = 256
NH = 16
HD = 128
EPS = 1e-6
NCORES = 8
BPC = 2
RING = 4
import os
ROPE_DEFER = os.environ.get('ROPE_DEFER', '1') == '1'
LOOKAHEAD = int(os.environ.get('LOOKAHEAD', '3'))
NDT = int(os.environ.get('NDT', '0'))
NTF = 6
NTB = 4
TT = 512

O_ADAB = 0
O_NG = O_ADAB + 96
O_FG = O_NG + 32
O_SINK = O_FG + 8
O_CB = O_SINK + 32
O_DWW = O_CB + 96
O_DWB = O_DWW + 992
O_LNG = O_DWB + 32
O_LNB = O_LNG + 32
NS = O_LNB + 32


class Prog:
    ENG = ("pe", "act", "dve", "pool", "sp")

    def __init__(self, dry):
        self.dry = dry
        self.ops = {e: [] for e in self.ENG}
        self.cnt = {}
        self.waited = {e: {} for e in self.ENG}
        self.lastw = {}
        self.readers = {}
        self.wsrc = []
        self.wpos = 0
        self.wissued = 0
        self.wlist = None
        self.rot = {}
        self.dead = set()

    def _deps(self, eng, reads, writes):
        need = {}

        def add(t):
            if t is None:
                return
            k, v = t
            if need.get(k, 0) < v:
                need[k] = v
        for r in reads:
            add(self.lastw.get(r))
            if isinstance(r, tuple) and r[0] == "ps":
                for k, v in self.readers.get(r, {}).items():
                    if k != eng:
                        add((k, v))
        for w in writes:
            add(self.lastw.get(w))
            for k, v in self.readers.get(w, {}).items():
                add((k, v))
        out = []
        wd = self.waited[eng]
        for k, v in need.items():
            if eng == "pe" and k == "pe":
                continue
            if wd.get(k, 0) >= v:
                continue
            wd[k] = v
            out.append((k, v))
        return out

    def _commit(self, tok, reads, writes):
        k, v = tok
        for r in reads:
            d = self.readers.setdefault(r, {})
            if d.get(k, 0) < v:
                d[k] = v
        for w in writes:
            self.lastw[w] = tok
            self.readers[w] = {}

    def op(self, eng, fn, reads=(), writes=()):
        if self.dry:
            return
        for r in reads:
            assert r not in self.dead, f"read of a dead (overwritten) resource {r}"
        waits = self._deps(eng, reads, writes)
        self.cnt[eng] = self.cnt.get(eng, 0) + 1
        tok = (eng, self.cnt[eng])
        self.ops[eng].append((waits, fn, eng, 1))
        self._commit(tok, reads, writes)

    def dma(self, eng, semkey, pairs, reads=(), writes=()):
        if self.dry:
            return
        waits = self._deps(eng, reads, writes)
        for i, (o, i_) in enumerate(pairs):
            self.cnt[semkey] = self.cnt.get(semkey, 0) + 16
            self.ops[eng].append((waits if i == 0 else [], (lambda e, o=o, i_=i_: e.dma_start(out=o, in_=i_)), semkey, 16))
        tok = (semkey, self.cnt[semkey])
        self._commit(tok, reads, writes)

    def barrier(self):
        if self.dry:
            return
        for e in self.ENG:
            waits = []
            for k, v in self.cnt.items():
                if k == e:
                    continue
                if self.waited[e].get(k, 0) < v:
                    self.waited[e][k] = v
                    waits.append((k, v))
            if waits:
                self.ops[e].append((waits, None, None, 0))
        self.lastw = {}
        self.readers = {}

    def rotate(self, name, n):
        i = self.rot.get(name, 0)
        self.rot[name] = (i + 1) % n
        return i


def _bc_mid(ap, n):
    a = ap.ap
    return bass.AP(ap.tensor, ap.offset, [list(a[0]), [0, n]] + [list(x) for x in a[1:]])


def _bc_last(ap, n):
    a = ap.ap
    return bass.AP(ap.tensor, ap.offset, [list(x) for x in a] + [[0, n]])


class _Stop(Exception):
    pass


def build_program(n_layers=4, final_norm=True, nb=BPC, stop=None, debug_ctx=False):
    nc = bass.Bass("TRN2", target_bir_lowering=False)
    dr = {}
    dr["xT"] = nc.dram_tensor("xT", [BPC, 8, 128, S], F32, kind="ExternalInput").ap()
    dr["cxT"] = nc.dram_tensor("cxT", [BPC, 8, 128, CL], F32, kind="ExternalInput").ap()
    dr["cT"] = nc.dram_tensor("cT", [128, 8, 3], F32, kind="ExternalInput").ap()
    dr["adaw"] = nc.dram_tensor("adaw", [4, 12, 128, 8 * 256], F32, kind="ExternalInput").ap()
    dr["small"] = nc.dram_tensor("small", [128, NS], F32, kind="ExternalInput").ap()
    dr["cb16"] = nc.dram_tensor("cb16", [128, 4 * 128], F32, kind="ExternalInput").ap()
    dr["rope"] = nc.dram_tensor("rope", [2, 128, S], F32, kind="ExternalInput").ap()
    dr["wa"] = nc.dram_tensor("wa", [2, 28, 128, 2048], F32, kind="ExternalInput").ap()
    dr["wc"] = nc.dram_tensor("wc", [2, 32, 128, 2048], F32, kind="ExternalInput").ap()
    dr["outT"] = nc.dram_tensor("outT", [BPC, 8, 128, S], F32, kind="ExternalOutput").ap()
    if stop is not None or debug_ctx:
        dr["cxo"] = nc.dram_tensor("cxo", [BPC, 8, 128, CL], F32, kind="ExternalOutput").ap()
    if stop is not None:
        dr["dH"] = nc.dram_tensor("dH", [128, 8 * 2304], BF16, kind="ExternalOutput").ap()
        dr["dLS"] = nc.dram_tensor("dLS", [128, 20864], BF16, kind="ExternalOutput").ap()
        dr["dLSF"] = nc.dram_tensor("dLSF", [128, 4096], F32, kind="ExternalOutput").ap()
        dr["dMOD"] = nc.dram_tensor("dMOD", [128, 3, 96], F32, kind="ExternalOutput").ap()

    es = ExitStack()
    with es:
        def sb(name, shape, dt):
            return es.enter_context(nc.sbuf_tensor(name, shape, dt))
        X = sb("X", [128, 8, S], F32)
        C = sb("C", [128, 8, CL], F32)
        WR = sb("WR", [128, RING, 2048], BF16)
        H = sb("H", [128, 8 * 2304], BF16)
        LSF = sb("LSF", [128, 4096], F32)
        LS = sb("LS", [128, 20864], BF16)
        TF = [sb(f"tf{i}", [128, TT], F32) for i in range(NTF)]
        TB = [sb(f"tb{i}", [128, TT], BF16) for i in range(NTB)]
        SM = sb("SM", [128, NS], F32)
        CB = sb("CB", [128, 4, 128], BF16)
        ONES = sb("ONES", [128, 128], BF16)
        MODA = sb("MODA", [128, 4, 8, 3], F32)
        MODS = sb("MODS", [128, 4, 8, 3], F32)
        MODG = sb("MODG", [128, 4, 8, 3], F32)
        MODRAW = sb("MODRAW", [128, 24, 3], F32)
        SC = sb("SC", [128, 8, 3], F32)
        SCT = sb("SCT", [128, 8, 3], F32)
        ESINK = sb("ESINK", [128, 32], F32)
        HB = sb("HB", [128, 96], F32)
        EPSD = sb("EPSD", [128, 2], F32)
        FGS = sb("FGS", [128, 8], F32)
        HLN = sb("HLN", [128, 64], F32)
        PSB = [es.enter_context(nc.psum_tensor(f"ps{i}", [128, TT], F32)) for i in range(8)]

        sem_names = list(Prog.ENG) + [f"wr{i}" for i in range(RING)] + ["ldx", "st", "cst", "cstb", "ada0", "ada1", "rope"]
        sems = {k: es.enter_context(nc.semaphore("s_" + k)) for k in sem_names}

        h_at = H[:, 0:8 * 2304].rearrange("p (k t) -> p k t", k=8)
        HCW = 1040
        h_cv = H[:, 0:8 * HCW].rearrange("p (k t) -> p k t", k=8)
        DG = H[:, 8 * HCW: 8 * HCW + 2 * 31 * 128].rearrange("p (b k j) -> p b k j", b=2, k=31)
        COS = LSF[:, 0:2048]
        SIN = LSF[:, 2048:4096]
        S1SB = LSF[:, 0:1024]
        RSTD = LSF[:, 1024:2048]
        KT = LS[:, 0:2304]
        VT = LS[:, 2304:4608].rearrange("p (b d) -> p b d", d=128)
        QT = LS[:, 4608:9216].rearrange("p (h t) -> p h t", h=2)
        SZ = LS[:, 9216:13824].rearrange("p (h t) -> p h t", h=2)
        OG = LS[:, 13824:18432].rearrange("p (h t) -> p h t", h=2)
        VC = LS[:, 0:16384].rearrange("p (c t) -> p c t", c=16)
        NG = LS[:, 16384:18432].rearrange("p (c t) -> p c t", c=2)
        UW = 1056
        UB = LS[:, 18432:18432 + 2 * UW].rearrange("p (b t) -> p b t", b=2)
        USV = LS[:, 18432 + 2 * UW: 18432 + 2 * UW + 256].rearrange("p (c t) -> p c t", c=16)

        RM = CB[:, 0, :]
        MPREV = CB[:, 1, :]
        MNEXT = CB[:, 2, :]
        IDN = CB[:, 3, :]

        def smv(off, n):
            return SM[:, off:off + n]

        def run(P):
            ATT_SCALE = float(HD) ** -0.5

            def chk(label):
                if stop is not None and label == stop:
                    raise _Stop()

            def ps_rot(lo, hi):
                return lo + P.rotate(("ps", lo, hi), hi - lo)

            def tf_next():
                i = 2 + P.rotate("tf", NTF - 2)
                return i, TF[i], ("tf", i)

            CTB = [TF[2], TF[3], TF[4], TF[5]] + [LSF[:, 2048 + 512 * i: 2048 + 512 * (i + 1)] for i in range(4)]

            def ct_next():
                i = P.rotate("ct", 8)
                return i, CTB[i], (("tf", 2 + i) if i < 4 else ("ctx", i))

            DGF = H[:, 8 * 1040: 8 * 1040 + 2 * 31 * 128].bitcast(F32)
            CT2 = CTB + [DGF[:, 512 * i: 512 * (i + 1)] for i in range(7)]
            dg_guard = set()

            def ct2_next():
                i = P.rotate("ct2", 15)
                if i < 4:
                    res = ("tf", 2 + i)
                elif i < 8:
                    res = ("ctx", i)
                else:
                    res = ("dgt", i)
                extra = []
                if i >= 8 and i not in dg_guard:
                    dg_guard.add(i)
                    extra = [("DG", 0), ("DG", 1)]
                return CT2[i], res, extra

            def rs_next():
                i = P.rotate("rs", 2)
                return i, TF[i], ("tf", i)

            def tb_next():
                i = P.rotate("tb", NTB)
                return i, TB[i], ("tb", i)

            def wslot(src):
                if P.dry:
                    P.wsrc.append(src)
                    return WR[:, 0, :], ("wr", 0)
                k = P.wpos
                P.wpos += 1
                while P.wissued < min(len(P.wlist), k + RING - 1):
                    jf = P.wissued
                    r = jf % RING
                    P.dma("pool", f"wr{r}", [(WR[:, r, :], P.wlist[jf])], reads=(), writes=[("wr", jf), ("wr", jf - RING)])
                    P.dead.add(("wr", jf - RING))
                    P.wissued += 1
                r = k % RING
                return WR[:, r, :], ("wr", k)

            def mm(out, pairs, reads, writes, flags=None):
                def fn(e):
                    n = len(pairs)
                    ins = None
                    for i, (l, r) in enumerate(pairs):
                        st, sp = (i == 0, i == n - 1) if flags is None else flags
                        ins = e.matmul(out, l, r, start=st, stop=sp, skip_group_check=True)
                    return ins
                P.op("pe", fn, reads, writes)

            def norm_mod(src, src_res, ncols, dst, dst_res, layer, row):
                n = ncols
                pb = ps_rot(0, 8)
                sqs = []
                for kc in range(8):
                    _, t, tr = tb_next()
                    if kc % 4 == 3:
                        P.op("act", (lambda e, t=t, kc=kc: e.activation(out=t[:, 0:n], in_=src[:, kc, :], func=AF.Square)),
                             reads=[src_res[kc]], writes=[tr])
                    else:
                        P.op("pool" if kc % 2 == 0 else "dve", (lambda e, t=t, kc=kc: e.tensor_tensor(out=t[:, 0:n], in0=src[:, kc, :], in1=src[:, kc, :], op=ALU.mult)),
                             reads=[src_res[kc]], writes=[tr])
                    mm(PSB[pb][:, 0:n], [(ONES[:, :], t[:, 0:n])], reads=[tr], writes=[("ps", pb)],
                       flags=(kc == 0, kc == 7))
                _, r1, r1r = tf_next()
                P.op("act", (lambda e: e.activation(out=r1[:, 0:n], in_=PSB[pb][:, 0:n], func=AF.Sqrt, bias=EPSD[:, 0:1])),
                     reads=[("ps", pb)], writes=[r1r])
                _, r2, r2r = rs_next()
                P.op("dve", (lambda e: e.reciprocal(out=r2[:, 0:n], in_=r1[:, 0:n])),
                     reads=[r1r], writes=[r2r])
                for kc in range(8):
                    _, t, tr = tf_next()
                    P.op("dve", (lambda e, t=t, kc=kc: e.tensor_tensor(out=t[:, 0:n], in0=src[:, kc, :], in1=r2[:, 0:n], op=ALU.mult)),
                         reads=[src_res[kc], r2r], writes=[tr])
                    if layer is None:
                        sc_ap = FGS[:, kc:kc + 1]
                        P.op("act", (lambda e, t=t, kc=kc, sc_ap=sc_ap: e.activation(out=dst[:, kc, :], in_=t[:, 0:n], func=AF.Identity, scale=sc_ap)),
                             reads=[tr], writes=[dst_res[kc]])
                    else:
                        sc_ap = MODA[:, layer, kc, row:row + 1]
                        bi_ap = MODS[:, layer, kc, row:row + 1]
                        P.op("act", (lambda e, t=t, kc=kc, sc_ap=sc_ap, bi_ap=bi_ap: e.activation(out=dst[:, kc, :], in_=t[:, 0:n], func=AF.Identity, scale=sc_ap, bias=bi_ap)),
                             reads=[tr], writes=[dst_res[kc]])

            XR = [[("X", kc, t) for kc in range(8)] for t in range(4)]
            CR = [("C", kc) for kc in range(8)]

            def load_x(b):
                P.dma("sp", "ldx", [(X[:, kc, :], dr["xT"][b, kc]) for kc in range(8)] + [(C[:, kc, :], dr["cxT"][b, kc]) for kc in range(8)],
                      writes=[r for t in range(4) for r in XR[t]] + CR)

            P.dma("sp", "cst", [(SM[:, :], dr["small"]), (SC[:, :, :], dr["cT"])], writes=["SM", "SC"])
            load_x(0)
            P.dma("pool", "cstb", [(CB[:, :, :].rearrange("p a b -> p (a b)"), dr["cb16"])], writes=["CB"])
            P.op("pool", lambda e: e.memset(ONES[:, :], 1.0), writes=["ONES"])
            P.op("pool", lambda e: e.memset(EPSD[:, 0:1], float(D * EPS)), writes=["EPSD"])
            P.op("pool", lambda e: e.memset(EPSD[:, 1:2], float(EPS)), reads=["EPSD"], writes=["EPSD"])
            P.op("act", lambda e: e.activation(out=SCT[:, :, :], in_=SC[:, :, :], func=AF.Tanh, scale=0.5), reads=["SC"], writes=["SCT"])
            P.op("dve", lambda e: e.scalar_tensor_tensor(out=SCT[:, :, :], in0=SCT[:, :, :], scalar=1.0, in1=SC[:, :, :], op0=ALU.add, op1=ALU.mult),
                 reads=["SC", "SCT"], writes=["SCT"])
            P.op("dve", lambda e: e.tensor_scalar(out=SCT[:, :, :], in0=SCT[:, :, :], scalar1=0.5, scalar2=None, op0=ALU.mult), reads=["SCT"], writes=["SCT"])
            P.op("act", lambda e: e.activation(out=ESINK[:, :], in_=smv(O_SINK, 32), func=AF.Exp), reads=["SM"], writes=["ESINK"])
            P.op("dve", lambda e: e.tensor_scalar(out=HB[:, :], in0=smv(O_CB, 96), scalar1=0.5, scalar2=None, op0=ALU.mult), reads=["SM"], writes=["HB"])
            P.op("dve", lambda e: e.tensor_scalar(out=FGS[:, :], in0=smv(O_FG, 8), scalar1=32.0, scalar2=None, op0=ALU.mult), reads=["SM"], writes=["FGS"])
            P.op("dve", lambda e: e.tensor_scalar(out=HLN[:, :], in0=smv(O_LNG, 64), scalar1=0.5, scalar2=None, op0=ALU.mult), reads=["SM"], writes=["HLN"])
            ADAV = [LSF[:, 0:2048].rearrange("p (k c) -> p k c", k=8), LSF[:, 2048:4096].rearrange("p (k c) -> p k c", k=8)]
            for l in range(n_layers):
                for pc in range(12):
                    bi = pc % 2
                    P.dma("sp", f"ada{bi}", [(LSF[:, bi * 2048:(bi + 1) * 2048], dr["adaw"][l, pc])], writes=[("ada", bi)])
                    for oc2 in range(2):
                        oc = pc * 2 + oc2
                        mm(PSB[0][:, oc * 3:oc * 3 + 3],
                           [(ADAV[bi][:, kc, oc2 * 128:(oc2 + 1) * 128], SCT[:, kc, :]) for kc in range(8)],
                           reads=[("ada", bi), "SCT"], writes=[("ps", 0)])
                adab = smv(O_ADAB + l * 24, 24)
                P.op("dve", (lambda e, adab=adab: e.tensor_tensor(out=MODRAW[:, :, :], in0=PSB[0][:, 0:72].rearrange("p (a b) -> p a b", b=3),
                                                                 in1=_bc_last(adab, 3), op=ALU.add)),
                     reads=[("ps", 0), "SM"], writes=["MODRAW"])
                ng = smv(O_NG + l * 8, 8)
                gsc = 0.5 if l % 2 == 0 else 0.25
                P.op("dve", (lambda e, l=l, ng=ng: e.scalar_tensor_tensor(out=MODA[:, l, :, :], in0=MODRAW[:, 8:16, :], scalar=1.0, in1=_bc_last(ng, 3), op0=ALU.add, op1=ALU.mult)),
                     reads=["MODRAW", "SM"], writes=[("MODA", l)])
                P.op("dve", (lambda e, l=l: e.tensor_scalar(out=MODA[:, l, :, :], in0=MODA[:, l, :, :], scalar1=32.0, scalar2=None, op0=ALU.mult)),
                     reads=[("MODA", l)], writes=[("MODA", l)])
                P.op("dve", (lambda e, l=l: e.tensor_copy(out=MODS[:, l, :, :], in_=MODRAW[:, 0:8, :])), reads=["MODRAW"], writes=[("MODS", l)])
                P.op("dve", (lambda e, l=l, gsc=gsc: e.tensor_scalar(out=MODG[:, l, :, :], in0=MODRAW[:, 16:24, :], scalar1=gsc, scalar2=None, op0=ALU.mult)),
                     reads=["MODRAW"], writes=[("MODG", l)])
            P.barrier()

            def attn_layer(b, l):
                j = l // 2
                ctx_out = any(k % 2 == 0 for k in range(l + 1, 4))
                P.dma("sp", "rope", [(COS, dr["rope"][0]), (SIN, dr["rope"][1])], writes=["ROPE"])
                tiles = [(0, CL)] + [(CL + TT * i, TT) for i in range(4)]
                HR = [[("H", kc, ti) for kc in range(8)] for ti in range(5)]
                for ti, (c0, n) in enumerate(tiles):
                    if ti == 0:
                        norm_mod(C[:, :, :], CR, n, h_at[:, :, c0:c0 + n], HR[ti], l, 2)
                    else:
                        norm_mod(X[:, :, (ti - 1) * TT: ti * TT], XR[ti - 1], n, h_at[:, :, c0:c0 + n], HR[ti], l, b)
                qtiles = list(range(5)) if ctx_out else list(range(1, 5))
                chk("norm")

                pend = []

                def rope_flush():
                    while pend:
                        pend.pop(0)()

                def rope_evac(pb, ti, dst, dst_res):
                    n = TT
                    t0 = (ti - 1) * TT
                    _, qb, qbr = tb_next()
                    P.op("act", lambda e: e.activation(out=qb[:, 0:n], in_=PSB[pb][:, 0:n], func=AF.Copy), reads=[("ps", pb)], writes=[qbr])

                    def rest():
                        pb2 = ps_rot(4, 8)
                        mm(PSB[pb2][:, 0:n], [(RM, qb[:, 0:n])], reads=[qbr], writes=[("ps", pb2)])
                        _, t1, t1r = tf_next()
                        P.op("dve", lambda e: e.tensor_tensor(out=t1[:, 0:n], in0=PSB[pb][:, 0:n], in1=COS[:, t0:t0 + n], op=ALU.mult),
                             reads=[("ps", pb), "ROPE"], writes=[t1r])
                        _, t2, t2r = tf_next()
                        P.op("dve", lambda e: e.tensor_tensor(out=t2[:, 0:n], in0=PSB[pb2][:, 0:n], in1=SIN[:, t0:t0 + n], op=ALU.mult),
                             reads=[("ps", pb2), "ROPE"], writes=[t2r])
                        P.op("pool", lambda e: e.tensor_tensor(out=dst, in0=t1[:, 0:n], in1=t2[:, 0:n], op=ALU.add), reads=[t1r, t2r], writes=[dst_res])
                    prev = list(pend)
                    del pend[:]
                    for f in prev:
                        f()
                    if ROPE_DEFER:
                        pend.append(rest)
                    else:
                        rest()

                for g in range(4):
                    kvs, kvr = wslot(dr["wa"][j, g * 7 + 0])
                    kv = kvs.rearrange("p (k c) -> p k c", k=8)
                    for ti, (c0, n) in enumerate(tiles):
                        pb = ps_rot(0, 4)
                        mm(PSB[pb][:, 0:n], [(kv[:, kc, 0:128], h_at[:, kc, c0:c0 + n]) for kc in range(8)],
                           reads=[kvr] + HR[ti], writes=[("ps", pb)])
                        if ti == 0:
                            P.op("act", (lambda e, pb=pb, c0=c0, n=n: e.activation(out=KT[:, c0:c0 + n], in_=PSB[pb][:, 0:n], func=AF.Copy)),
                                 reads=[("ps", pb)], writes=[("KT", ti)])
                        else:
                            rope_evac(pb, ti, KT[:, c0:c0 + n], ("KT", ti))
                        chk(f"k{ti}")
                        pb = ps_rot(0, 4)
                        nblk = n // 128
                        pairs = []

                        def vfn(e, pb=pb, c0=c0, nblk=nblk, kv=kv):
                            ins = None
                            for bk in range(nblk):
                                for kc in range(8):
                                    ins = e.matmul(PSB[pb][:, bk * 128:(bk + 1) * 128], h_at[:, kc, c0 + bk * 128: c0 + (bk + 1) * 128], kv[:, kc, 128:256],
                                                   start=(kc == 0), stop=(kc == 7), skip_group_check=True)
                            return ins
                        P.op("pe", vfn, reads=[kvr] + HR[ti], writes=[("ps", pb)])
                        b0 = c0 // 128
                        P.op("act", (lambda e, pb=pb, b0=b0, nblk=nblk, n=n: e.activation(out=VT[:, b0:b0 + nblk, :], in_=PSB[pb][:, 0:n].rearrange("p (b d) -> p b d", d=128), func=AF.Copy)),
                             reads=[("ps", pb)], writes=[("VT", ti)])
                        chk(f"v{ti}")
                    rope_flush()
                    chk("kv")
                    for pr in range(2):
                        qs, qr = wslot(dr["wa"][j, g * 7 + 1 + pr * 3])
                        qv = qs.rearrange("p (k c) -> p k c", k=8)
                        for hh in range(2):
                            for ti in qtiles:
                                c0, n = tiles[ti]
                                pb = ps_rot(0, 4)
                                mm(PSB[pb][:, 0:n], [(qv[:, kc, hh * 128:(hh + 1) * 128], h_at[:, kc, c0:c0 + n]) for kc in range(8)],
                                   reads=[qr] + HR[ti], writes=[("ps", pb)])
                                if ti == 0:
                                    P.op("act", (lambda e, pb=pb, c0=c0, n=n, hh=hh: e.activation(out=QT[:, hh, c0:c0 + n], in_=PSB[pb][:, 0:n], func=AF.Copy)),
                                         reads=[("ps", pb)], writes=[("QT", hh, ti)])
                                else:
                                    rope_evac(pb, ti, QT[:, hh, c0:c0 + n], ("QT", hh, ti))
                        rope_flush()
                        zs, zr = wslot(dr["wa"][j, g * 7 + 2 + pr * 3])
                        zv = zs.rearrange("p (k c) -> p k c", k=8)
                        for hh in range(2):
                            for ti in qtiles:
                                c0, n = tiles[ti]
                                pb = ps_rot(0, 4)
                                mm(PSB[pb][:, 0:n], [(zv[:, kc, hh * 128:(hh + 1) * 128], h_at[:, kc, c0:c0 + n]) for kc in range(8)],
                                   reads=[zr] + HR[ti], writes=[("ps", pb)])
                                _, t, tr = tf_next()
                                P.op("act", (lambda e, pb=pb, n=n, t=t: e.activation(out=t[:, 0:n], in_=PSB[pb][:, 0:n], func=AF.Tanh, scale=0.5)),
                                     reads=[("ps", pb)], writes=[tr])
                                P.op("dve", (lambda e, pb=pb, n=n, t=t, hh=hh, c0=c0: e.scalar_tensor_tensor(out=SZ[:, hh, c0:c0 + n], in0=t[:, 0:n], scalar=1.0, in1=PSB[pb][:, 0:n], op0=ALU.add, op1=ALU.mult)),
                                     reads=[("ps", pb), tr], writes=[("SZ", hh, ti)])
                        chk("qz")
                        for hh in range(2):
                            head = 4 * g + 2 * pr + hh
                            for ti in qtiles:
                                c0, n = tiles[ti]
                                otb = 4 + P.rotate("ot", 2)
                                lb = 6 + P.rotate("lb", 2)
                                kblocks = [(0, 0, n, None, None), (1, 0, n, None, None)]
                                if ti > 0:
                                    qb0 = 4 * (ti - 1)
                                    for kb in range(qb0 - 1, qb0 + 5):
                                        if kb < 0 or kb >= 16:
                                            continue
                                        qlo = max(qb0, kb - 1)
                                        qhi = min(qb0 + 3, kb + 1)
                                        kblocks.append((2 + kb, (qlo - qb0) * 128, (qhi - qlo + 1) * 128, kb, qb0))
                                nk = len(kblocks)
                                pts = {}

                                def emit_s(bi_):
                                    vb, cc0, cn, kb, qb0 = kblocks[bi_]
                                    sbk = ps_rot(0, 4)
                                    kti = 0 if vb < 2 else 1 + (vb - 2) // 4
                                    masks = []
                                    if kb is not None:
                                        for qb in range(qb0 + cc0 // 128, qb0 + (cc0 + cn) // 128):
                                            m = MPREV if qb == kb + 1 else (MNEXT if qb == kb - 1 else None)
                                            if m is not None:
                                                masks.append(((qb - qb0) * 128, m))

                                    def sfn(e, sbk=sbk, vb=vb, cc0=cc0, cn=cn, masks=masks, hh=hh, c0=c0):
                                        nm = len(masks)
                                        ins = e.matmul(PSB[sbk][:, cc0:cc0 + cn], KT[:, vb * 128:(vb + 1) * 128], QT[:, hh, c0 + cc0:c0 + cc0 + cn],
                                                       start=True, stop=(nm == 0), skip_group_check=True)
                                        for mi, (o_, m) in enumerate(masks):
                                            ins = e.matmul(PSB[sbk][:, o_:o_ + 128], IDN, m, start=False, stop=(mi == nm - 1), skip_group_check=True)
                                        return ins
                                    P.op("pe", sfn, reads=[("KT", kti), ("QT", hh, ti)], writes=[("ps", sbk)])
                                    _, pt, ptr = tb_next()
                                    pts[bi_] = (pt, ptr)
                                    P.op("act", (lambda e, sbk=sbk, cc0=cc0, cn=cn, pt=pt: e.activation(out=pt[:, cc0:cc0 + cn], in_=PSB[sbk][:, cc0:cc0 + cn], func=AF.Exp, scale=ATT_SCALE)),
                                         reads=[("ps", sbk)], writes=[ptr])

                                def emit_pv(bi_):
                                    vb, cc0, cn, kb, qb0 = kblocks[bi_]
                                    kti = 0 if vb < 2 else 1 + (vb - 2) // 4
                                    pt, ptr = pts.pop(bi_)
                                    mm(PSB[otb][:, cc0:cc0 + cn], [(VT[:, vb, :], pt[:, cc0:cc0 + cn])], reads=[("VT", kti), ptr], writes=[("ps", otb)],
                                       flags=(bi_ == 0, bi_ == nk - 1))
                                    mm(PSB[lb][:, cc0:cc0 + cn], [(ONES[:, :], pt[:, cc0:cc0 + cn])], reads=[ptr], writes=[("ps", lb)],
                                       flags=(bi_ == 0, bi_ == nk - 1))

                                LOOK = LOOKAHEAD
                                for bi_ in range(min(LOOK, nk)):
                                    emit_s(bi_)
                                for bi_ in range(nk):
                                    emit_pv(bi_)
                                    if bi_ + LOOK < nk:
                                        emit_s(bi_ + LOOK)
                                _, d1, d1r = tf_next()
                                es_ap = ESINK[:, j * 16 + head: j * 16 + head + 1]
                                P.op("act", (lambda e, lb=lb, n=n, d1=d1, es_ap=es_ap: e.activation(out=d1[:, 0:n], in_=PSB[lb][:, 0:n], func=AF.Identity, bias=es_ap)),
                                     reads=[("ps", lb)], writes=[d1r])
                                _, d2, d2r = tf_next()
                                P.op("dve", (lambda e, n=n, d1=d1, d2=d2: e.reciprocal(out=d2[:, 0:n], in_=d1[:, 0:n])), reads=[d1r], writes=[d2r])
                                _, d3, d3r = tf_next()
                                P.op("pool", (lambda e, n=n, d2=d2, d3=d3, hh=hh, c0=c0: e.tensor_tensor(out=d3[:, 0:n], in0=d2[:, 0:n], in1=SZ[:, hh, c0:c0 + n], op=ALU.mult)),
                                     reads=[d2r, ("SZ", hh, ti)], writes=[d3r])
                                P.op("dve", (lambda e, n=n, d3=d3, hh=hh, c0=c0, otb=otb: e.tensor_tensor(out=OG[:, hh, c0:c0 + n], in0=PSB[otb][:, 0:n], in1=d3[:, 0:n], op=ALU.mult)),
                                     reads=[("ps", otb), d3r], writes=[("OG", hh, ti)])
                        ws_, wr_ = wslot(dr["wa"][j, g * 7 + 3 + pr * 3])
                        wv = ws_.rearrange("p (k c) -> p k c", k=2)
                        for ti in qtiles:
                            def emit_wout(ti=ti, wv=wv, wr_=wr_):
                                c0, n = tiles[ti]
                                for mc in range(8):
                                    pb = ps_rot(0, 4)
                                    mm(PSB[pb][:, 0:n], [(wv[:, h2, mc * 128:(mc + 1) * 128], OG[:, h2, c0:c0 + n]) for h2 in range(2)],
                                       reads=[wr_, ("OG", 0, ti), ("OG", 1, ti)], writes=[("ps", pb)])
                                    if ti == 0:
                                        dst = C[:, mc, :]
                                        dres = CR[mc]
                                        gt = MODG[:, l, mc, 2:3]
                                    else:
                                        dst = X[:, mc, (ti - 1) * TT: ti * TT]
                                        dres = XR[ti - 1][mc]
                                        gt = MODG[:, l, mc, b:b + 1]
                                    if mc % 4 == 2:
                                        _, tu, tur = tf_next()
                                        P.op("act", (lambda e, pb=pb, n=n, tu=tu, gt=gt: e.activation(out=tu[:, 0:n], in_=PSB[pb][:, 0:n], func=AF.Identity, scale=gt)),
                                             reads=[("ps", pb)], writes=[tur])
                                        P.op("pool", (lambda e, n=n, tu=tu, dst=dst: e.tensor_tensor(out=dst, in0=dst, in1=tu[:, 0:n], op=ALU.add)),
                                             reads=[tur, dres], writes=[dres])
                                    else:
                                        P.op("dve", (lambda e, pb=pb, n=n, dst=dst, gt=gt: e.scalar_tensor_tensor(out=dst, in0=PSB[pb][:, 0:n], scalar=gt, in1=dst, op0=ALU.mult, op1=ALU.add)),
                                             reads=[("ps", pb), dres], writes=[dres])
                            emit_wout()
                P.barrier()

            def conv_layer(b, l):
                j = l // 2
                ctx_out = any(k % 2 == 0 for k in range(l + 1, 4))
                ranges = ([("ctx", 0, CL)] if ctx_out else []) + [("lat", 0, 1024), ("lat", 1024, 1024)]
                for (kind, r0, rl) in ranges:
                    row = 2 if kind == "ctx" else b
                    halo_r = (kind == "lat" and r0 == 0)
                    halo_l = (kind == "lat" and r0 > 0)
                    rt = []
                    if kind == "ctx":
                        rt.append((0, CL, C[:, :, :], CR))
                    else:
                        for i in range(2):
                            t = r0 // TT + i
                            rt.append((i * TT, TT, X[:, :, t * TT:(t + 1) * TT], XR[t]))
                        if halo_r:
                            rt.append((1024, 15, X[:, :, 1024:1039], XR[2]))
                    HR = [[("H", kc, ti) for kc in range(8)] for ti in range(len(rt))]
                    for ti, (c0, n, sv, sr) in enumerate(rt):
                        norm_mod(sv, sr, n, h_cv[:, :, c0:c0 + n], HR[ti], l, row)
                    outt = [x for x in rt if x[1] > 15]
                    nst = len(outt)
                    def p1_front(cc):
                            ws_, wr_ = wslot(dr["wc"][j, cc])
                            wv = ws_.rearrange("p (k c) -> p k c", k=8)
                            ub = P.rotate("ub", 2)
                            U = UB[:, ub, :]
                            ures = ("U", ub)
                            dww = smv(O_DWW + (j * 16 + cc) * 31, 31)
                            P.op("dve", (lambda e, ub=ub, dww=dww: e.tensor_tensor(out=DG[:, ub, :, :], in0=_bc_mid(IDN, 31), in1=_bc_last(dww, 128), op=ALU.mult)),
                                 reads=[], writes=[("DG", ub)])
                            if halo_l:
                                P.op("pool", (lambda e, U=U, cc=cc: e.tensor_copy(out=U[:, 0:15], in_=USV[:, cc, 0:15])), reads=[("USV", cc)], writes=[ures])
                            else:
                                P.op("pool", (lambda e, U=U: e.memset(U[:, 0:15], 0.0)), writes=[ures])
                            if not halo_r:
                                P.op("pool", (lambda e, U=U, rl=rl: e.memset(U[:, 15 + rl:15 + rl + 15], 0.0)), reads=[ures], writes=[ures])
                            for ti, (c0, n, sv, sr) in enumerate(rt):
                                pa = ps_rot(0, 4)
                                mm(PSB[pa][:, 0:n], [(wv[:, kc, 0:128], h_cv[:, kc, c0:c0 + n]) for kc in range(8)], reads=[wr_] + HR[ti], writes=[("ps", pa)])
                                pg = ps_rot(0, 4)
                                mm(PSB[pg][:, 0:n], [(wv[:, kc, 128:256], h_cv[:, kc, c0:c0 + n]) for kc in range(8)], reads=[wr_] + HR[ti], writes=[("ps", pg)])
                                _, t1, t1r = tf_next()
                                hbg = HB[:, j * 48 + 16 + cc: j * 48 + 16 + cc + 1]
                                hba = HB[:, j * 48 + cc: j * 48 + cc + 1]
                                P.op("act", (lambda e, pg=pg, n=n, t1=t1, hbg=hbg: e.activation(out=t1[:, 0:n], in_=PSB[pg][:, 0:n], func=AF.Tanh, scale=0.5, bias=hbg)),
                                     reads=[("ps", pg)], writes=[t1r])
                                _, t2, t2r = tf_next()
                                P.op("act", (lambda e, pa=pa, n=n, t2=t2, hba=hba: e.activation(out=t2[:, 0:n], in_=PSB[pa][:, 0:n], func=AF.Identity, scale=0.5, bias=hba)),
                                     reads=[("ps", pa)], writes=[t2r])
                                P.op("dve", (lambda e, n=n, t1=t1, t2=t2, U=U, c0=c0: e.scalar_tensor_tensor(out=U[:, 15 + c0:15 + c0 + n], in0=t1[:, 0:n], scalar=1.0, in1=t2[:, 0:n], op0=ALU.add, op1=ALU.mult)),
                                     reads=[t1r, t2r, ures], writes=[ures])
                            if halo_r:
                                P.op("pool", (lambda e, U=U, cc=cc: e.tensor_copy(out=USV[:, cc, 0:15], in_=U[:, 15 + 1009:15 + 1024])), reads=[ures], writes=[("USV", cc)])
                            return ub, U, ures

                    def p1_back(cc, ub, U, ures):
                            dwb = smv(O_DWB + j * 16 + cc, 1)
                            accs = []
                            for si, (c0, n, sv, sr) in enumerate(outt):
                                if NDT > 0:
                                    _, ac, acr = ct_next()
                                else:
                                    ac, acr = None, None
                                accs.append((ac, acr))
                            for k in range(NDT):
                                wk = smv(O_DWW + (j * 16 + cc) * 31 + k, 1)
                                for si, (c0, n, sv, sr) in enumerate(outt):
                                    ac, acr = accs[si]
                                    if k == 0:
                                        P.op("dve", (lambda e, ac=ac, n=n, U=U, c0=c0, k=k, wk=wk: e.tensor_scalar(out=ac[:, 0:n], in0=U[:, c0 + k:c0 + k + n], scalar1=wk, scalar2=None, op0=ALU.mult)),
                                             reads=[ures], writes=[acr])
                                    else:
                                        P.op("dve", (lambda e, ac=ac, n=n, U=U, c0=c0, k=k, wk=wk: e.scalar_tensor_tensor(out=ac[:, 0:n], in0=U[:, c0 + k:c0 + k + n], scalar=wk, in1=ac[:, 0:n], op0=ALU.mult, op1=ALU.add)),
                                             reads=[ures, acr], writes=[acr])
                            for si, (c0, n, sv, sr) in enumerate(outt):
                                ac, acr = accs[si]
                                pv = ps_rot(0, 4)
                                mm(PSB[pv][:, 0:n], [(DG[:, ub, k, :], U[:, c0 + k:c0 + k + n]) for k in range(NDT, 31)], reads=[("DG", ub), ures], writes=[("ps", pv)])
                                if NDT > 0:
                                    P.op("dve", (lambda e, pv=pv, n=n, ac=ac: e.tensor_tensor(out=ac[:, 0:n], in0=PSB[pv][:, 0:n], in1=ac[:, 0:n], op=ALU.add)),
                                         reads=[("ps", pv), acr], writes=[acr])
                                    vsrc = ac
                                    vres = acr
                                else:
                                    vsrc = PSB[pv]
                                    vres = ("ps", pv)
                                P.op("act", (lambda e, vsrc=vsrc, n=n, cc=cc, c0=c0, dwb=dwb: e.activation(out=VC[:, cc, c0:c0 + n], in_=vsrc[:, 0:n], func=AF.Identity, bias=dwb)),
                                     reads=[vres], writes=[("VC", cc, si)])
                                _, sq, sqr = tb_next()
                                P.op("act", (lambda e, vsrc=vsrc, n=n, sq=sq, dwb=dwb: e.activation(out=sq[:, 0:n], in_=vsrc[:, 0:n], func=AF.Square, bias=dwb)),
                                     reads=[vres], writes=[sqr])
                                mm(PSB[4 + si][:, 0:n], [(ONES[:, :], VC[:, cc, c0:c0 + n])], reads=[("VC", cc, si)], writes=[("ps", 4 + si)], flags=(cc == 0, cc == 15))
                                mm(PSB[6 + si][:, 0:n], [(ONES[:, :], sq[:, 0:n])], reads=[sqr], writes=[("ps", 6 + si)], flags=(cc == 0, cc == 15))

                    fctx = p1_front(0)
                    for cc in range(16):
                        nxt = p1_front(cc + 1) if cc + 1 < 16 else None
                        p1_back(cc, *fctx)
                        fctx = nxt
                    for si, (c0, n, sv, sr) in enumerate(outt):
                        P.op("dve", (lambda e, si=si, c0=c0, n=n: e.tensor_scalar(out=S1SB[:, c0:c0 + n], in0=PSB[4 + si][:, 0:n], scalar1=1.0 / 2048, scalar2=None, op0=ALU.mult)),
                             reads=[("ps", 4 + si)], writes=[("S1SB", si)])
                        _, t1, t1r = tf_next()
                        P.op("dve", (lambda e, c0=c0, n=n, t1=t1: e.tensor_tensor(out=t1[:, 0:n], in0=S1SB[:, c0:c0 + n], in1=S1SB[:, c0:c0 + n], op=ALU.mult)),
                             reads=[("S1SB", si)], writes=[t1r])
                        _, t2, t2r = tf_next()
                        P.op("dve", (lambda e, si=si, n=n, t1=t1, t2=t2: e.scalar_tensor_tensor(out=t2[:, 0:n], in0=PSB[6 + si][:, 0:n], scalar=1.0 / 2048, in1=t1[:, 0:n], op0=ALU.mult, op1=ALU.subtract)),
                             reads=[("ps", 6 + si), t1r], writes=[t2r])
                        _, t3, t3r = tf_next()
                        P.op("act", (lambda e, n=n, t2=t2, t3=t3: e.activation(out=t3[:, 0:n], in_=t2[:, 0:n], func=AF.Sqrt, bias=EPSD[:, 1:2])),
                             reads=[t2r], writes=[t3r])
                        P.op("dve", (lambda e, c0=c0, n=n, t3=t3: e.reciprocal(out=RSTD[:, c0:c0 + n], in_=t3[:, 0:n])),
                             reads=[t3r], writes=[("RSTD", si)])
                    cpend = []
                    s2pend = []
                    dg_guard.clear()
                    for pz in range(8):
                        zs, zr = wslot(dr["wc"][j, 16 + pz * 2])
                        zv = zs.rearrange("p (k c) -> p k c", k=8)
                        woh = {}
                        for si, (c0, n, sv, sr) in enumerate(outt):
                            keep = []
                            for tag, f in cpend:
                                if tag == si:
                                    f()
                                else:
                                    keep.append((tag, f))
                            cpend[:] = keep
                            for ci in range(2):
                                cc = pz * 2 + ci
                                bz = smv(O_CB + j * 48 + 32 + cc, 1)
                                lng = smv(O_LNG + j * 16 + cc, 1)
                                lnb = smv(O_LNB + j * 16 + cc, 1)
                                hbz = HB[:, j * 48 + 32 + cc: j * 48 + 32 + cc + 1]
                                hlg = HLN[:, j * 16 + cc: j * 16 + cc + 1]
                                hlb = HLN[:, 32 + j * 16 + cc: 32 + j * 16 + cc + 1]
                                tB, tBr, xB = ct2_next()
                                tC, tCr, xC = ct2_next()
                                P.op("dve", (lambda e, n=n, tB=tB, cc=cc, c0=c0: e.tensor_tensor(out=tB[:, 0:n], in0=VC[:, cc, c0:c0 + n], in1=S1SB[:, c0:c0 + n], op=ALU.subtract)),
                                     reads=[("VC", cc, si), ("S1SB", si)], writes=[tBr] + xB)
                                P.op("pool", (lambda e, n=n, tB=tB, c0=c0: e.tensor_tensor(out=tB[:, 0:n], in0=tB[:, 0:n], in1=RSTD[:, c0:c0 + n], op=ALU.mult)),
                                     reads=[tBr, ("RSTD", si)], writes=[tBr])
                                P.op("act", (lambda e, n=n, tB=tB, tC=tC, hlg=hlg, hlb=hlb: e.activation(out=tC[:, 0:n], in_=tB[:, 0:n], func=AF.Tanh, scale=hlg, bias=hlb)),
                                     reads=[tBr], writes=[tCr] + xC)
                                P.op("act", (lambda e, n=n, tB=tB, lng=lng, lnb=lnb: e.activation(out=tB[:, 0:n], in_=tB[:, 0:n], func=AF.Identity, scale=lng, bias=lnb)),
                                     reads=[tBr], writes=[tBr])

                                def stage2(n=n, c0=c0, si=si, ci=ci, cc=cc, tB=tB, tBr=tBr, tC=tC, tCr=tCr, bz=bz, hbz=hbz, zv=zv, zr=zr, HRs=HR[si]):
                                    tA, tAr, xA = ct2_next()
                                    tD, tDr, xD = ct2_next()
                                    P.op("dve", (lambda e: e.scalar_tensor_tensor(out=tC[:, 0:n], in0=tC[:, 0:n], scalar=1.0, in1=tB[:, 0:n], op0=ALU.add, op1=ALU.mult)),
                                         reads=[tBr, tCr], writes=[tCr])
                                    pzb = ps_rot(0, 8)
                                    mm(PSB[pzb][:, 0:n], [(zv[:, kc, ci * 128:(ci + 1) * 128], h_cv[:, kc, c0:c0 + n]) for kc in range(8)], reads=[zr] + HRs, writes=[("ps", pzb)])
                                    P.op("act", (lambda e: e.activation(out=tA[:, 0:n], in_=PSB[pzb][:, 0:n], func=AF.Identity, bias=bz)),
                                         reads=[("ps", pzb)], writes=[tAr] + xA)
                                    P.op("act", (lambda e: e.activation(out=tD[:, 0:n], in_=PSB[pzb][:, 0:n], func=AF.Tanh, scale=0.5, bias=hbz)),
                                         reads=[("ps", pzb)], writes=[tDr] + xD)
                                    P.op("dve", (lambda e: e.scalar_tensor_tensor(out=tA[:, 0:n], in0=tD[:, 0:n], scalar=1.0, in1=tA[:, 0:n], op0=ALU.add, op1=ALU.mult)),
                                         reads=[tAr, tDr], writes=[tAr])
                                    P.op("pool", (lambda e: e.tensor_tensor(out=NG[:, ci, c0:c0 + n], in0=tC[:, 0:n], in1=tA[:, 0:n], op=ALU.mult)),
                                         reads=[tAr, tCr], writes=[("NG", ci, si)])
                                    if cpend:
                                        cpend.pop(0)[1]()
                                for f in s2pend:
                                    f()
                                del s2pend[:]
                                s2pend.append(stage2)
                            for f in s2pend:
                                f()
                            del s2pend[:]
                            def emit_cwout(mcs, si=si, c0=c0, n=n, sv=sv, sr=sr, woh=woh, pz=pz):
                                if "w" not in woh:
                                    woh["w"] = wslot(dr["wc"][j, 16 + pz * 2 + 1])
                                ws_, wr_ = woh["w"]
                                wv = ws_.rearrange("p (k c) -> p k c", k=2)
                                for mc in mcs:
                                    pb = ps_rot(0, 8)
                                    mm(PSB[pb][:, 0:n], [(wv[:, ci2, mc * 128:(mc + 1) * 128], NG[:, ci2, c0:c0 + n]) for ci2 in range(2)],
                                       reads=[wr_, ("NG", 0, si), ("NG", 1, si)], writes=[("ps", pb)])
                                    dst = sv[:, mc, :]
                                    gt = MODG[:, l, mc, row:row + 1]
                                    if mc % 4 == 2:
                                        tu, tur, xu = ct2_next()
                                        P.op("act", (lambda e, pb=pb, n=n, tu=tu, gt=gt: e.activation(out=tu[:, 0:n], in_=PSB[pb][:, 0:n], func=AF.Identity, scale=gt)),
                                             reads=[("ps", pb)], writes=[tur] + xu)
                                        P.op("pool", (lambda e, n=n, tu=tu, dst=dst: e.tensor_tensor(out=dst, in0=dst, in1=tu[:, 0:n], op=ALU.add)),
                                             reads=[tur, sr[mc]], writes=[sr[mc]])
                                    else:
                                        P.op("dve", (lambda e, pb=pb, n=n, dst=dst, gt=gt: e.scalar_tensor_tensor(out=dst, in0=PSB[pb][:, 0:n], scalar=gt, in1=dst, op0=ALU.mult, op1=ALU.add)),
                                             reads=[("ps", pb), sr[mc]], writes=[sr[mc]])
                            cpend.append((si, lambda f=emit_cwout: f(range(0, 4))))
                            cpend.append((si, lambda f=emit_cwout: f(range(4, 8))))
                    for tag, f in cpend:
                        f()
                    del cpend[:]
                    P.barrier()

            for b in range(nb):
                if b > 0:
                    load_x(b)
                try:
                    for l in range(n_layers):
                        if l % 2 == 0:
                            attn_layer(b, l)
                        else:
                            conv_layer(b, l)
                except _Stop:
                    P.barrier()
                    P.dma("sp", "st", [(dr["dH"], H[:, :]), (dr["dLS"], LS[:, :]), (dr["dLSF"], LSF[:, :]),
                                       (dr["dMOD"][:, 0, :], MODA[:, :, :, :].rearrange("p a b c -> p (a b c)")),
                                       (dr["dMOD"][:, 1, :], MODS[:, :, :, :].rearrange("p a b c -> p (a b c)")),
                                       (dr["dMOD"][:, 2, :], MODG[:, :, :, :].rearrange("p a b c -> p (a b c)"))])
                if final_norm:
                    for t in range(4):
                        xs = X[:, :, t * TT:(t + 1) * TT]
                        norm_mod(xs, XR[t], TT, xs, XR[t], None, None)
                P.dma("sp", "st", [(dr["outT"][b, kc], X[:, kc, :]) for kc in range(8)]
                      + ([(dr["cxo"][b, kc], C[:, kc, :]) for kc in range(8)] if "cxo" in dr else []),
                      reads=[r for t in range(4) for r in XR[t]] + CR)
            P.barrier()

        Pd = Prog(dry=True)
        run(Pd)
        P = Prog(dry=False)
        P.wlist = Pd.wsrc
        run(P)
        global _LAST_PROG
        _LAST_PROG = P

        with nc.Block() as block:
            def replay(eng, e):
                for waits, fn, semkey, inc in P.ops[eng]:
                    for k, v in waits:
                        e.wait_ge(sems[k], v)
                    if fn is not None:
                        ins = fn(e)
                        ins.then_inc(sems[semkey], inc)

            @block.tensor
            def _(e):
                replay("pe", e)

            @block.scalar
            def _(e):
                replay("act", e)

            @block.vector
            def _(e):
                replay("dve", e)

            @block.gpsimd
            def _(e):
                replay("pool", e)

            @block.sync
            def _(e):
                replay("sp", e)
    return nc


def _rope_tables():
    rows = S // 64
    row = np.repeat(np.arange(rows), 64).astype(np.float32)
    col = np.tile(np.arange(64), rows).astype(np.float32)
    n_axis = HD // 4
    inv = (10000.0 ** (-np.arange(n_axis, dtype=np.float32) / n_axis)).astype(np.float32)
    ang = np.concatenate([row[:, None] * inv, col[:, None] * inv], axis=-1).astype(np.float32)
    cos = np.cos(ang).astype(np.float32).T
    sin = np.sin(ang).astype(np.float32).T
    return np.ascontiguousarray(np.stack([np.concatenate([cos, cos], 0), np.concatenate([sin, sin], 0)], 0))


def _const16():
    rm = np.zeros((128, 128), np.float32)
    for m in range(64):
        rm[m + 64, m] = -1.0
        rm[m, m + 64] = 1.0
    b = np.arange(128)[:, None]
    a = np.arange(128)[None, :]
    mprev = np.where(b >= a, 0.0, -30000.0).astype(np.float32)
    mnext = np.where(b <= a, 0.0, -30000.0).astype(np.float32)
    idn = np.eye(128, dtype=np.float32)
    return np.ascontiguousarray(np.stack([rm, mprev, mnext, idn], 1).reshape(128, 512))


def _pk(v):
    return np.ascontiguousarray(v.reshape(-1, 128).T)


def _wslot_k8(w, cols):
    s = w[:, cols].reshape(8, 128, len(cols)).transpose(1, 0, 2)
    return s.reshape(128, -1)


def _wslot_out(w, r0):
    s = w[r0:r0 + 256].reshape(2, 128, 1024).transpose(1, 0, 2)
    return s.reshape(128, -1)


def _prep_shared(inp):
    f = np.float32
    ada_w = np.asarray(inp["ada_w"], f)
    adaw = np.ascontiguousarray(ada_w.reshape(4, 8, 128, 12, 256).transpose(0, 3, 2, 1, 4).reshape(4, 12, 128, 2048))
    small = np.zeros((128, NS), f)
    for l in range(4):
        small[:, O_ADAB + l * 24:O_ADAB + (l + 1) * 24] = _pk(np.asarray(inp["ada_b"], f)[l])
        small[:, O_NG + l * 8:O_NG + (l + 1) * 8] = _pk(np.asarray(inp["norm_g"], f)[l])
    small[:, O_FG:O_FG + 8] = _pk(np.asarray(inp["final_g"], f))
    small[:, O_SINK:O_SINK + 32] = np.asarray(inp["attn_sink"], f).reshape(1, 32)
    for j in range(2):
        small[:, O_CB + j * 48:O_CB + (j + 1) * 48] = _pk(np.asarray(inp["conv_b_in"], f)[j])
        dw = np.asarray(inp["conv_dw_w"], f)[j]
        small[:, O_DWW + j * 496:O_DWW + (j + 1) * 496] = dw.T.reshape(16, 128, 31).transpose(1, 0, 2).reshape(128, 496)
        small[:, O_DWB + j * 16:O_DWB + (j + 1) * 16] = _pk(np.asarray(inp["conv_dw_b"], f)[j])
        small[:, O_LNG + j * 16:O_LNG + (j + 1) * 16] = _pk(np.asarray(inp["conv_ln_g"], f)[j])
        small[:, O_LNB + j * 16:O_LNB + (j + 1) * 16] = _pk(np.asarray(inp["conv_ln_b"], f)[j])
    wa = np.zeros((2, 28, 128, 2048), f)
    awi = np.asarray(inp["attn_w_in"], f)
    awo = np.asarray(inp["attn_w_out"], f)
    for j in range(2):
        for g in range(4):
            kvc = list(range(2048 + g * 128, 2048 + (g + 1) * 128)) + list(range(2560 + g * 128, 2560 + (g + 1) * 128))
            wa[j, g * 7 + 0] = _wslot_k8(awi[j], kvc)
            for pr in range(2):
                q0 = g * 512 + pr * 256
                wa[j, g * 7 + 1 + pr * 3] = _wslot_k8(awi[j], list(range(q0, q0 + 256)))
                wa[j, g * 7 + 2 + pr * 3] = _wslot_k8(awi[j], list(range(3072 + q0, 3072 + q0 + 256)))
                wa[j, g * 7 + 3 + pr * 3] = _wslot_out(awo[j], q0)
    wc = np.zeros((2, 32, 128, 2048), f)
    cwi = np.asarray(inp["conv_w_in"], f)
    cwo = np.asarray(inp["conv_w_out"], f)
    for j in range(2):
        for cc in range(16):
            cols = list(range(cc * 128, (cc + 1) * 128)) + list(range(2048 + cc * 128, 2048 + (cc + 1) * 128))
            wc[j, cc] = _wslot_k8(cwi[j], cols)
        for pz in range(8):
            wc[j, 16 + pz * 2] = _wslot_k8(cwi[j], list(range(4096 + pz * 256, 4096 + (pz + 1) * 256)))
            wc[j, 16 + pz * 2 + 1] = _wslot_out(cwo[j], pz * 256)
    return {"adaw": adaw, "small": small, "cb16": _const16(), "rope": _rope_tables(), "wa": wa, "wc": wc}


def _prep_core(inp, shared, core):
    f = np.float32
    b0 = core * BPC
    x = np.asarray(inp["x"], f)[b0:b0 + BPC]
    ctx = np.asarray(inp["ctx"], f)[b0:b0 + BPC]
    xT = np.ascontiguousarray(x.transpose(0, 2, 1).reshape(BPC, 8, 128, S))
    cxT = np.ascontiguousarray(ctx.transpose(0, 2, 1).reshape(BPC, 8, 128, CL))
    c = np.asarray(inp["c"], f)[b0:b0 + BPC]
    rows = np.concatenate([c, np.asarray(inp["c_ctx"], f)[None, :]], 0)
    cT = np.ascontiguousarray(rows.reshape(3, 8, 128).transpose(2, 1, 0))
    m = {"xT": xT, "cxT": cxT, "cT": cT}
    m.update(shared)
    return m


_NC_CACHE = {}


def kernel(**inputs):
    if "nc" not in _NC_CACHE:
        _NC_CACHE["nc"] = build_program()
    nc = _NC_CACHE["nc"]
    shared = _prep_shared(inputs)
    in_maps = [_prep_core(inputs, shared, c) for c in range(NCORES)]
    res = run_bass_kernel_spmd(nc, in_maps, core_ids=list(range(NCORES)))
    outs = []
    for c in range(NCORES):
        o = np.asarray(res.results[c]["outT"], np.float32).reshape(BPC, D, S)
        outs.append(o.transpose(0, 2, 1))
    return np.ascontiguousarray(np.concatenate(outs, 0)).astype(np.float32)
```

```python
import numpy as np
from contextlib import ExitStack
import concourse.bass as bass
import concourse.mybir as mybir
from concourse.bass_utils import run_bass_kernel_spmd

F32 = mybir.dt.float32
BF16 = mybir.dt.bfloat16
AF = mybir.ActivationFunctionType
ALU = mybir.AluOpType

D = 1024
S = 2048
CL = 256
NH = 16
HD = 128
EPS = 1e-6
NCORES = 8
BPC = 2
RING = 4
import os
ROPE_DEFER = os.environ.get('ROPE_DEFER', '1') == '1'
LOOKAHEAD = int(os.environ.get('LOOKAHEAD', '3'))
NDT = int(os.environ.get('NDT', '0'))
NTF = 6
NTB = 4
TT = 512

O_ADAB = 0
O_NG = O_ADAB + 96
O_FG = O_NG + 32
O_SINK = O_FG + 8
O_CB = O_SINK + 32
O_DWW = O_CB + 96
O_DWB = O_DWW + 992
O_LNG = O_DWB + 32
O_LNB = O_LNG + 32
NS = O_LNB + 32


class Prog:
    ENG = ("pe", "act", "dve", "pool", "sp")

    def __init__(self, dry):
        self.dry = dry
        self.ops = {e: [] for e in self.ENG}
        self.cnt = {}
        self.waited = {e: {} for e in self.ENG}
        self.lastw = {}
        self.readers = {}
        self.wsrc = []
        self.wpos = 0
        self.wissued = 0
        self.wlist = None
        self.rot = {}
        self.dead = set()

    def _deps(self, eng, reads, writes):
        need = {}

        def add(t):
            if t is None:
                return
            k, v = t
            if need.get(k, 0) < v:
                need[k] = v
        for r in reads:
            add(self.lastw.get(r))
            if isinstance(r, tuple) and r[0] == "ps":
                for k, v in self.readers.get(r, {}).items():
                    if k != eng:
                        add((k, v))
        for w in writes:
            add(self.lastw.get(w))
            for k, v in self.readers.get(w, {}).items():
                add((k, v))
        out = []
        wd = self.waited[eng]
        for k, v in need.items():
            if eng == "pe" and k == "pe":
                continue
            if wd.get(k, 0) >= v:
                continue
            wd[k] = v
            out.append((k, v))
        return out

    def _commit(self, tok, reads, writes):
        k, v = tok
        for r in reads:
            d = self.readers.setdefault(r, {})
            if d.get(k, 0) < v:
                d[k] = v
        for w in writes:
            self.lastw[w] = tok
            self.readers[w] = {}

    def op(self, eng, fn, reads=(), writes=()):
        if self.dry:
            return
        for r in reads:
            assert r not in self.dead, f"read of a dead (overwritten) resource {r}"
        waits = self._deps(eng, reads, writes)
        self.cnt[eng] = self.cnt.get(eng, 0) + 1
        tok = (eng, self.cnt[eng])
        self.ops[eng].append((waits, fn, eng, 1))
        self._commit(tok, reads, writes)

    def dma(self, eng, semkey, pairs, reads=(), writes=()):
        if self.dry:
            return
        waits = self._deps(eng, reads, writes)
        for i, (o, i_) in enumerate(pairs):
            self.cnt[semkey] = self.cnt.get(semkey, 0) + 16
            self.ops[eng].append((waits if i == 0 else [], (lambda e, o=o, i_=i_: e.dma_start(out=o, in_=i_)), semkey, 16))
        tok = (semkey, self.cnt[semkey])
        self._commit(tok, reads, writes)

    def barrier(self):
        if self.dry:
            return
        for e in self.ENG:
            waits = []
            for k, v in self.cnt.items():
                if k == e:
                    continue
                if self.waited[e].get(k, 0) < v:
                    self.waited[e][k] = v
                    waits.append((k, v))
            if waits:
                self.ops[e].append((waits, None, None, 0))
        self.lastw = {}
        self.readers = {}

    def rotate(self, name, n):
        i = self.rot.get(name, 0)
        self.rot[name] = (i + 1) % n
        return i


def _bc_mid(ap, n):
    a = ap.ap
    return bass.AP(ap.tensor, ap.offset, [list(a[0]), [0, n]] + [list(x) for x in a[1:]])


def _bc_last(ap, n):
    a = ap.ap
    return bass.AP(ap.tensor, ap.offset, [list(x) for x in a] + [[0, n]])


class _Stop(Exception):
    pass


def build_program(n_layers=4, final_norm=True, nb=BPC, stop=None, debug_ctx=False):
    nc = bass.Bass("TRN2", target_bir_lowering=False)
    dr = {}
    dr["xT"] = nc.dram_tensor("xT", [BPC, 8, 128, S], F32, kind="ExternalInput").ap()
    dr["cxT"] = nc.dram_tensor("cxT", [BPC, 8, 128, CL], F32, kind="ExternalInput").ap()
    dr["cT"] = nc.dram_tensor("cT", [128, 8, 3], F32, kind="ExternalInput").ap()
    dr["adaw"] = nc.dram_tensor("adaw", [4, 12, 128, 8 * 256], F32, kind="ExternalInput").ap()
    dr["small"] = nc.dram_tensor("small", [128, NS], F32, kind="ExternalInput").ap()
    dr["cb16"] = nc.dram_tensor("cb16", [128, 4 * 128], F32, kind="ExternalInput").ap()
    dr["rope"] = nc.dram_tensor("rope", [2, 128, S], F32, kind="ExternalInput").ap()
    dr["wa"] = nc.dram_tensor("wa", [2, 28, 128, 2048], F32, kind="ExternalInput").ap()
    dr["wc"] = nc.dram_tensor("wc", [2, 32, 128, 2048], F32, kind="ExternalInput").ap()
    dr["outT"] = nc.dram_tensor("outT", [BPC, 8, 128, S], F32, kind="ExternalOutput").ap()
    if stop is not None or debug_ctx:
        dr["cxo"] = nc.dram_tensor("cxo", [BPC, 8, 128, CL], F32, kind="ExternalOutput").ap()
    if stop is not None:
        dr["dH"] = nc.dram_tensor("dH", [128, 8 * 2304], BF16, kind="ExternalOutput").ap()
        dr["dLS"] = nc.dram_tensor("dLS", [128, 20864], BF16, kind="ExternalOutput").ap()
        dr["dLSF"] = nc.dram_tensor("dLSF", [128, 4096], F32, kind="ExternalOutput").ap()
        dr["dMOD"] = nc.dram_tensor("dMOD", [128, 3, 96], F32, kind="ExternalOutput").ap()

    es = ExitStack()
    with es:
        def sb(name, shape, dt):
            return es.enter_context(nc.sbuf_tensor(name, shape, dt))
        X = sb("X", [128, 8, S], F32)
        C = sb("C", [128, 8, CL], F32)
        WR = sb("WR", [128, RING, 2048], BF16)
        H = sb("H", [128, 8 * 2304], BF16)
        LSF = sb("LSF", [128, 4096], F32)
        LS = sb("LS", [128, 20864], BF16)
        TF = [sb(f"tf{i}", [128, TT], F32) for i in range(NTF)]
        TB = [sb(f"tb{i}", [128, TT], BF16) for i in range(NTB)]
        SM = sb("SM", [128, NS], F32)
        CB = sb("CB", [128, 4, 128], BF16)
        ONES = sb("ONES", [128, 128], BF16)
        MODA = sb("MODA", [128, 4, 8, 3], F32)
        MODS = sb("MODS", [128, 4, 8, 3], F32)
        MODG = sb("MODG", [128, 4, 8, 3], F32)
        MODRAW = sb("MODRAW", [128, 24, 3], F32)
        SC = sb("SC", [128, 8, 3], F32)
        SCT = sb("SCT", [128, 8, 3], F32)
        ESINK = sb("ESINK", [128, 32], F32)
        HB = sb("HB", [128, 96], F32)
        EPSD = sb("EPSD", [128, 2], F32)
        FGS = sb("FGS", [128, 8], F32)
        HLN = sb("HLN", [128, 64], F32)
        PSB = [es.enter_context(nc.psum_tensor(f"ps{i}", [128, TT], F32)) for i in range(8)]

        sem_names = list(Prog.ENG) + [f"wr{i}" for i in range(RING)] + ["ldx", "st", "cst", "cstb", "ada0", "ada1", "rope"]
        sems = {k: es.enter_context(nc.semaphore("s_" + k)) for k in sem_names}

        h_at = H[:, 0:8 * 2304].rearrange("p (k t) -> p k t", k=8)
        HCW = 1040
        h_cv = H[:, 0:8 * HCW].rearrange("p (k t) -> p k t", k=8)
        DG = H[:, 8 * HCW: 8 * HCW + 2 * 31 * 128].rearrange("p (b k j) -> p b k j", b=2, k=31)
        COS = LSF[:, 0:2048]
        SIN = LSF[:, 2048:4096]
        S1SB = LSF[:, 0:1024]
        RSTD = LSF[:, 1024:2048]
        KT = LS[:, 0:2304]
        VT = LS[:, 2304:4608].rearrange("p (b d) -> p b d", d=128)
        QT = LS[:, 4608:9216].rearrange("p (h t) -> p h t", h=2)
        SZ = LS[:, 9216:13824].rearrange("p (h t) -> p h t", h=2)
        OG = LS[:, 13824:18432].rearrange("p (h t) -> p h t", h=2)
        VC = LS[:, 0:16384].rearrange("p (c t) -> p c t", c=16)
        NG = LS[:, 16384:18432].rearrange("p (c t) -> p c t", c=2)
        UW = 1056
        UB = LS[:, 18432:18432 + 2 * UW].rearrange("p (b t) -> p b t", b=2)
        USV = LS[:, 18432 + 2 * UW: 18432 + 2 * UW + 256].rearrange("p (c t) -> p c t", c=16)

        RM = CB[:, 0, :]
        MPREV = CB[:, 1, :]
        MNEXT = CB[:, 2, :]
        IDN = CB[:, 3, :]

        def smv(off, n):
            return SM[:, off:off + n]

        def run(P):
            ATT_SCALE = float(HD) ** -0.5

            def chk(label):
                if stop is not None and label == stop:
                    raise _Stop()

            live_banks = set()

            def ps_rot(lo, hi):
                bk = lo + P.rotate(("ps", lo, hi), hi - lo)
                assert bk not in live_banks, f"PSUM bank {bk} handed out while a deferred reader is pending"
                return bk

            def tf_next():
                i = 2 + P.rotate("tf", NTF - 2)
                return i, TF[i], ("tf", i)

            CTB = [TF[2], TF[3], TF[4], TF[5]] + [LSF[:, 2048 + 512 * i: 2048 + 512 * (i + 1)] for i in range(4)]

            def ct_next():
                i = P.rotate("ct", 8)
                return i, CTB[i], (("tf", 2 + i) if i < 4 else ("ctx", i))

            DGF = H[:, 8 * 1040: 8 * 1040 + 2 * 31 * 128].bitcast(F32)
            CT2 = CTB + [DGF[:, 512 * i: 512 * (i + 1)] for i in range(7)]
            dg_guard = set()

            def ct2_next():
                i = P.rotate("ct2", 15)
                if i < 4:
                    res = ("tf", 2 + i)
                elif i < 8:
                    res = ("ctx", i)
                else:
                    res = ("dgt", i)
                extra = []
                if i >= 8 and i not in dg_guard:
                    dg_guard.add(i)
                    extra = [("DG", 0), ("DG", 1)]
                return CT2[i], res, extra

            def rs_next():
                i = P.rotate("rs", 2)
                return i, TF[i], ("tf", i)

            def tb_next():
                i = P.rotate("tb", NTB)
                return i, TB[i], ("tb", i)

            def wslot(src):
                if P.dry:
                    P.wsrc.append(src)
                    return WR[:, 0, :], ("wr", 0)
                k = P.wpos
                P.wpos += 1
                while P.wissued < min(len(P.wlist), k + RING - 1):
                    jf = P.wissued
                    r = jf % RING
                    P.dma("pool", f"wr{r}", [(WR[:, r, :], P.wlist[jf])], reads=(), writes=[("wr", jf), ("wr", jf - RING)])
                    P.dead.add(("wr", jf - RING))
                    P.wissued += 1
                r = k % RING
                return WR[:, r, :], ("wr", k)

            def mm(out, pairs, reads, writes, flags=None):
                def fn(e):
                    n = len(pairs)
                    ins = None
                    for i, (l, r) in enumerate(pairs):
                        st, sp = (i == 0, i == n - 1) if flags is None else flags
                        ins = e.matmul(out, l, r, start=st, stop=sp, skip_group_check=True)
                    return ins
                P.op("pe", fn, reads, writes)

            def norm_mod(src, src_res, ncols, dst, dst_res, layer, row):
                n = ncols
                pb = ps_rot(0, 8)
                sqs = []
                for kc in range(8):
                    _, t, tr = tb_next()
                    if kc % 4 == 3:
                        P.op("act", (lambda e, t=t, kc=kc: e.activation(out=t[:, 0:n], in_=src[:, kc, :], func=AF.Square)),
                             reads=[src_res[kc]], writes=[tr])
                    else:
                        P.op("pool" if kc % 2 == 0 else "dve", (lambda e, t=t, kc=kc: e.tensor_tensor(out=t[:, 0:n], in0=src[:, kc, :], in1=src[:, kc, :], op=ALU.mult)),
                             reads=[src_res[kc]], writes=[tr])
                    mm(PSB[pb][:, 0:n], [(ONES[:, :], t[:, 0:n])], reads=[tr], writes=[("ps", pb)],
                       flags=(kc == 0, kc == 7))
                _, r1, r1r = tf_next()
                P.op("act", (lambda e: e.activation(out=r1[:, 0:n], in_=PSB[pb][:, 0:n], func=AF.Sqrt, bias=EPSD[:, 0:1])),
                     reads=[("ps", pb)], writes=[r1r])
                _, r2, r2r = rs_next()
                P.op("dve", (lambda e: e.reciprocal(out=r2[:, 0:n], in_=r1[:, 0:n])),
                     reads=[r1r], writes=[r2r])
                for kc in range(8):
                    _, t, tr = tf_next()
                    P.op("dve", (lambda e, t=t, kc=kc: e.tensor_tensor(out=t[:, 0:n], in0=src[:, kc, :], in1=r2[:, 0:n], op=ALU.mult)),
                         reads=[src_res[kc], r2r], writes=[tr])
                    if layer is None:
                        sc_ap = FGS[:, kc:kc + 1]
                        P.op("act", (lambda e, t=t, kc=kc, sc_ap=sc_ap: e.activation(out=dst[:, kc, :], in_=t[:, 0:n], func=AF.Identity, scale=sc_ap)),
                             reads=[tr], writes=[dst_res[kc]])
                    else:
                        sc_ap = MODA[:, layer, kc, row:row + 1]
                        bi_ap = MODS[:, layer, kc, row:row + 1]
                        P.op("act", (lambda e, t=t, kc=kc, sc_ap=sc_ap, bi_ap=bi_ap: e.activation(out=dst[:, kc, :], in_=t[:, 0:n], func=AF.Identity, scale=sc_ap, bias=bi_ap)),
                             reads=[tr], writes=[dst_res[kc]])

            XR = [[("X", kc, t) for kc in range(8)] for t in range(4)]
            CR = [("C", kc) for kc in range(8)]

            def load_x(b):
                P.dma("sp", "ldx", [(X[:, kc, :], dr["xT"][b, kc]) for kc in range(8)] + [(C[:, kc, :], dr["cxT"][b, kc]) for kc in range(8)],
                      writes=[r for t in range(4) for r in XR[t]] + CR)

            P.dma("sp", "cst", [(SM[:, :], dr["small"]), (SC[:, :, :], dr["cT"])], writes=["SM", "SC"])
            load_x(0)
            P.dma("pool", "cstb", [(CB[:, :, :].rearrange("p a b -> p (a b)"), dr["cb16"])], writes=["CB"])
            P.op("pool", lambda e: e.memset(ONES[:, :], 1.0), writes=["ONES"])
            P.op("pool", lambda e: e.memset(EPSD[:, 0:1], float(D * EPS)), writes=["EPSD"])
            P.op("pool", lambda e: e.memset(EPSD[:, 1:2], float(EPS)), reads=["EPSD"], writes=["EPSD"])
            P.op("act", lambda e: e.activation(out=SCT[:, :, :], in_=SC[:, :, :], func=AF.Tanh, scale=0.5), reads=["SC"], writes=["SCT"])
            P.op("dve", lambda e: e.scalar_tensor_tensor(out=SCT[:, :, :], in0=SCT[:, :, :], scalar=1.0, in1=SC[:, :, :], op0=ALU.add, op1=ALU.mult),
                 reads=["SC", "SCT"], writes=["SCT"])
            P.op("dve", lambda e: e.tensor_scalar(out=SCT[:, :, :], in0=SCT[:, :, :], scalar1=0.5, scalar2=None, op0=ALU.mult), reads=["SCT"], writes=["SCT"])
            P.op("act", lambda e: e.activation(out=ESINK[:, :], in_=smv(O_SINK, 32), func=AF.Exp), reads=["SM"], writes=["ESINK"])
            P.op("dve", lambda e: e.tensor_scalar(out=HB[:, :], in0=smv(O_CB, 96), scalar1=0.5, scalar2=None, op0=ALU.mult), reads=["SM"], writes=["HB"])
            P.op("dve", lambda e: e.tensor_scalar(out=FGS[:, :], in0=smv(O_FG, 8), scalar1=32.0, scalar2=None, op0=ALU.mult), reads=["SM"], writes=["FGS"])
            P.op("dve", lambda e: e.tensor_scalar(out=HLN[:, :], in0=smv(O_LNG, 64), scalar1=0.5, scalar2=None, op0=ALU.mult), reads=["SM"], writes=["HLN"])
            ADAV = [LSF[:, 0:2048].rearrange("p (k c) -> p k c", k=8), LSF[:, 2048:4096].rearrange("p (k c) -> p k c", k=8)]
            for l in range(n_layers):
                for pc in range(12):
                    bi = pc % 2
                    P.dma("sp", f"ada{bi}", [(LSF[:, bi * 2048:(bi + 1) * 2048], dr["adaw"][l, pc])], writes=[("ada", bi)])
                    for oc2 in range(2):
                        oc = pc * 2 + oc2
                        mm(PSB[0][:, oc * 3:oc * 3 + 3],
                           [(ADAV[bi][:, kc, oc2 * 128:(oc2 + 1) * 128], SCT[:, kc, :]) for kc in range(8)],
                           reads=[("ada", bi), "SCT"], writes=[("ps", 0)])
                adab = smv(O_ADAB + l * 24, 24)
                P.op("dve", (lambda e, adab=adab: e.tensor_tensor(out=MODRAW[:, :, :], in0=PSB[0][:, 0:72].rearrange("p (a b) -> p a b", b=3),
                                                                 in1=_bc_last(adab, 3), op=ALU.add)),
                     reads=[("ps", 0), "SM"], writes=["MODRAW"])
                ng = smv(O_NG + l * 8, 8)
                gsc = 0.5 if l % 2 == 0 else 0.25
                P.op("dve", (lambda e, l=l, ng=ng: e.scalar_tensor_tensor(out=MODA[:, l, :, :], in0=MODRAW[:, 8:16, :], scalar=1.0, in1=_bc_last(ng, 3), op0=ALU.add, op1=ALU.mult)),
                     reads=["MODRAW", "SM"], writes=[("MODA", l)])
                P.op("dve", (lambda e, l=l: e.tensor_scalar(out=MODA[:, l, :, :], in0=MODA[:, l, :, :], scalar1=32.0, scalar2=None, op0=ALU.mult)),
                     reads=[("MODA", l)], writes=[("MODA", l)])
                P.op("dve", (lambda e, l=l: e.tensor_copy(out=MODS[:, l, :, :], in_=MODRAW[:, 0:8, :])), reads=["MODRAW"], writes=[("MODS", l)])
                P.op("dve", (lambda e, l=l, gsc=gsc: e.tensor_scalar(out=MODG[:, l, :, :], in0=MODRAW[:, 16:24, :], scalar1=gsc, scalar2=None, op0=ALU.mult)),
                     reads=["MODRAW"], writes=[("MODG", l)])
            P.barrier()

            def attn_layer(b, l):
                j = l // 2
                ctx_out = any(k % 2 == 0 for k in range(l + 1, 4))
                P.dma("sp", "rope", [(COS, dr["rope"][0]), (SIN, dr["rope"][1])], writes=["ROPE"])
                tiles = [(0, CL)] + [(CL + TT * i, TT) for i in range(4)]
                HR = [[("H", kc, ti) for kc in range(8)] for ti in range(5)]
                for ti, (c0, n) in enumerate(tiles):
                    if ti == 0:
                        norm_mod(C[:, :, :], CR, n, h_at[:, :, c0:c0 + n], HR[ti], l, 2)
                    else:
                        norm_mod(X[:, :, (ti - 1) * TT: ti * TT], XR[ti - 1], n, h_at[:, :, c0:c0 + n], HR[ti], l, b)
                qtiles = list(range(5)) if ctx_out else list(range(1, 5))
                chk("norm")

                pend = []

                def rope_flush():
                    while pend:
                        pend.pop(0)()

                def rope_evac(pb, ti, dst, dst_res):
                    n = TT
                    t0 = (ti - 1) * TT
                    _, qb, qbr = tb_next()
                    P.op("act", lambda e: e.activation(out=qb[:, 0:n], in_=PSB[pb][:, 0:n], func=AF.Copy), reads=[("ps", pb)], writes=[qbr])

                    live_banks.add(pb)

                    def rest():
                        live_banks.discard(pb)
                        pb2 = ps_rot(4, 8)
                        mm(PSB[pb2][:, 0:n], [(RM, qb[:, 0:n])], reads=[qbr], writes=[("ps", pb2)])
                        _, t1, t1r = tf_next()
                        P.op("dve", lambda e: e.tensor_tensor(out=t1[:, 0:n], in0=PSB[pb][:, 0:n], in1=COS[:, t0:t0 + n], op=ALU.mult),
                             reads=[("ps", pb), "ROPE"], writes=[t1r])
                        _, t2, t2r = tf_next()
                        P.op("dve", lambda e: e.tensor_tensor(out=t2[:, 0:n], in0=PSB[pb2][:, 0:n], in1=SIN[:, t0:t0 + n], op=ALU.mult),
                             reads=[("ps", pb2), "ROPE"], writes=[t2r])
                        P.op("pool", lambda e: e.tensor_tensor(out=dst, in0=t1[:, 0:n], in1=t2[:, 0:n], op=ALU.add), reads=[t1r, t2r], writes=[dst_res])
                    prev = list(pend)
                    del pend[:]
                    for f in prev:
                        f()
                    if ROPE_DEFER:
                        pend.append(rest)
                    else:
                        rest()

                wout_q = []

                def pop_wout(k):
                    while wout_q and k > 0:
                        wout_q.pop(0)()
                        k -= 1

                for g in range(4):
                    kvs, kvr = wslot(dr["wa"][j, g * 7 + 0])
                    kv = kvs.rearrange("p (k c) -> p k c", k=8)
                    for ti, (c0, n) in enumerate(tiles):
                        pb = ps_rot(0, 4)
                        mm(PSB[pb][:, 0:n], [(kv[:, kc, 0:128], h_at[:, kc, c0:c0 + n]) for kc in range(8)],
                           reads=[kvr] + HR[ti], writes=[("ps", pb)])
                        if ti == 0:
                            P.op("act", (lambda e, pb=pb, c0=c0, n=n: e.activation(out=KT[:, c0:c0 + n], in_=PSB[pb][:, 0:n], func=AF.Copy)),
                                 reads=[("ps", pb)], writes=[("KT", ti)])
                        else:
                            rope_evac(pb, ti, KT[:, c0:c0 + n], ("KT", ti))
                        chk(f"k{ti}")
                        pb = ps_rot(0, 4)
                        nblk = n // 128
                        pairs = []

                        def vfn(e, pb=pb, c0=c0, nblk=nblk, kv=kv):
                            ins = None
                            for bk in range(nblk):
                                for kc in range(8):
                                    ins = e.matmul(PSB[pb][:, bk * 128:(bk + 1) * 128], h_at[:, kc, c0 + bk * 128: c0 + (bk + 1) * 128], kv[:, kc, 128:256],
                                                   start=(kc == 0), stop=(kc == 7), skip_group_check=True)
                            return ins
                        P.op("pe", vfn, reads=[kvr] + HR[ti], writes=[("ps", pb)])
                        b0 = c0 // 128
                        P.op("act", (lambda e, pb=pb, b0=b0, nblk=nblk, n=n: e.activation(out=VT[:, b0:b0 + nblk, :], in_=PSB[pb][:, 0:n].rearrange("p (b d) -> p b d", d=128), func=AF.Copy)),
                             reads=[("ps", pb)], writes=[("VT", ti)])
                        chk(f"v{ti}")
                    rope_flush()
                    chk("kv")
                    for pr in range(2):
                        qs, qr = wslot(dr["wa"][j, g * 7 + 1 + pr * 3])
                        qv = qs.rearrange("p (k c) -> p k c", k=8)
                        for hh in range(2):
                            for ti in qtiles:
                                c0, n = tiles[ti]
                                if ti == 0:
                                    rope_flush()
                                pb = ps_rot(0, 4)
                                mm(PSB[pb][:, 0:n], [(qv[:, kc, hh * 128:(hh + 1) * 128], h_at[:, kc, c0:c0 + n]) for kc in range(8)],
                                   reads=[qr] + HR[ti], writes=[("ps", pb)])
                                if ti == 0:
                                    P.op("act", (lambda e, pb=pb, c0=c0, n=n, hh=hh: e.activation(out=QT[:, hh, c0:c0 + n], in_=PSB[pb][:, 0:n], func=AF.Copy)),
                                         reads=[("ps", pb)], writes=[("QT", hh, ti)])
                                else:
                                    rope_evac(pb, ti, QT[:, hh, c0:c0 + n], ("QT", hh, ti))
                                pop_wout(2)
                        rope_flush()
                        zs, zr = wslot(dr["wa"][j, g * 7 + 2 + pr * 3])
                        zv = zs.rearrange("p (k c) -> p k c", k=8)
                        for hh in range(2):
                            for ti in qtiles:
                                c0, n = tiles[ti]
                                pb = ps_rot(0, 4)
                                mm(PSB[pb][:, 0:n], [(zv[:, kc, hh * 128:(hh + 1) * 128], h_at[:, kc, c0:c0 + n]) for kc in range(8)],
                                   reads=[zr] + HR[ti], writes=[("ps", pb)])
                                _, t, tr = tf_next()
                                P.op("act", (lambda e, pb=pb, n=n, t=t: e.activation(out=t[:, 0:n], in_=PSB[pb][:, 0:n], func=AF.Tanh, scale=0.5)),
                                     reads=[("ps", pb)], writes=[tr])
                                P.op("dve", (lambda e, pb=pb, n=n, t=t, hh=hh, c0=c0: e.scalar_tensor_tensor(out=SZ[:, hh, c0:c0 + n], in0=t[:, 0:n], scalar=1.0, in1=PSB[pb][:, 0:n], op0=ALU.add, op1=ALU.mult)),
                                     reads=[("ps", pb), tr], writes=[("SZ", hh, ti)])
                                pop_wout(2)
                        pop_wout(10 ** 6)
                        chk("qz")
                        for hh in range(2):
                            head = 4 * g + 2 * pr + hh
                            for ti in qtiles:
                                c0, n = tiles[ti]
                                otb = 4 + P.rotate("ot", 2)
                                lb = 6 + P.rotate("lb", 2)
                                kblocks = [(0, 0, n, None, None), (1, 0, n, None, None)]
                                if ti > 0:
                                    qb0 = 4 * (ti - 1)
                                    for kb in range(qb0 - 1, qb0 + 5):
                                        if kb < 0 or kb >= 16:
                                            continue
                                        qlo = max(qb0, kb - 1)
                                        qhi = min(qb0 + 3, kb + 1)
                                        kblocks.append((2 + kb, (qlo - qb0) * 128, (qhi - qlo + 1) * 128, kb, qb0))
                                nk = len(kblocks)
                                pts = {}

                                def emit_s(bi_):
                                    vb, cc0, cn, kb, qb0 = kblocks[bi_]
                                    sbk = ps_rot(0, 4)
                                    kti = 0 if vb < 2 else 1 + (vb - 2) // 4
                                    masks = []
                                    if kb is not None:
                                        for qb in range(qb0 + cc0 // 128, qb0 + (cc0 + cn) // 128):
                                            m = MPREV if qb == kb + 1 else (MNEXT if qb == kb - 1 else None)
                                            if m is not None:
                                                masks.append(((qb - qb0) * 128, m))

                                    def sfn(e, sbk=sbk, vb=vb, cc0=cc0, cn=cn, masks=masks, hh=hh, c0=c0):
                                        nm = len(masks)
                                        ins = e.matmul(PSB[sbk][:, cc0:cc0 + cn], KT[:, vb * 128:(vb + 1) * 128], QT[:, hh, c0 + cc0:c0 + cc0 + cn],
                                                       start=True, stop=(nm == 0), skip_group_check=True)
                                        for mi, (o_, m) in enumerate(masks):
                                            ins = e.matmul(PSB[sbk][:, o_:o_ + 128], IDN, m, start=False, stop=(mi == nm - 1), skip_group_check=True)
                                        return ins
                                    P.op("pe", sfn, reads=[("KT", kti), ("QT", hh, ti)], writes=[("ps", sbk)])
                                    _, pt, ptr = tb_next()
                                    pts[bi_] = (pt, ptr)
                                    P.op("act", (lambda e, sbk=sbk, cc0=cc0, cn=cn, pt=pt: e.activation(out=pt[:, cc0:cc0 + cn], in_=PSB[sbk][:, cc0:cc0 + cn], func=AF.Exp, scale=ATT_SCALE)),
                                         reads=[("ps", sbk)], writes=[ptr])

                                def emit_pv(bi_):
                                    vb, cc0, cn, kb, qb0 = kblocks[bi_]
                                    kti = 0 if vb < 2 else 1 + (vb - 2) // 4
                                    pt, ptr = pts.pop(bi_)
                                    mm(PSB[otb][:, cc0:cc0 + cn], [(VT[:, vb, :], pt[:, cc0:cc0 + cn])], reads=[("VT", kti), ptr], writes=[("ps", otb)],
                                       flags=(bi_ == 0, bi_ == nk - 1))
                                    mm(PSB[lb][:, cc0:cc0 + cn], [(ONES[:, :], pt[:, cc0:cc0 + cn])], reads=[ptr], writes=[("ps", lb)],
                                       flags=(bi_ == 0, bi_ == nk - 1))

                                LOOK = LOOKAHEAD
                                for bi_ in range(min(LOOK, nk)):
                                    emit_s(bi_)
                                for bi_ in range(nk):
                                    emit_pv(bi_)
                                    if bi_ + LOOK < nk:
                                        emit_s(bi_ + LOOK)
                                _, d1, d1r = tf_next()
                                es_ap = ESINK[:, j * 16 + head: j * 16 + head + 1]
                                P.op("act", (lambda e, lb=lb, n=n, d1=d1, es_ap=es_ap: e.activation(out=d1[:, 0:n], in_=PSB[lb][:, 0:n], func=AF.Identity, bias=es_ap)),
                                     reads=[("ps", lb)], writes=[d1r])
                                _, d2, d2r = tf_next()
                                P.op("dve", (lambda e, n=n, d1=d1, d2=d2: e.reciprocal(out=d2[:, 0:n], in_=d1[:, 0:n])), reads=[d1r], writes=[d2r])
                                _, d3, d3r = tf_next()
                                P.op("pool", (lambda e, n=n, d2=d2, d3=d3, hh=hh, c0=c0: e.tensor_tensor(out=d3[:, 0:n], in0=d2[:, 0:n], in1=SZ[:, hh, c0:c0 + n], op=ALU.mult)),
                                     reads=[d2r, ("SZ", hh, ti)], writes=[d3r])
                                P.op("dve", (lambda e, n=n, d3=d3, hh=hh, c0=c0, otb=otb: e.tensor_tensor(out=OG[:, hh, c0:c0 + n], in0=PSB[otb][:, 0:n], in1=d3[:, 0:n], op=ALU.mult)),
                                     reads=[("ps", otb), d3r], writes=[("OG", hh, ti)])
                        woh = {}
                        for ti in qtiles:
                            for mc in range(8):
                                def emit_wout1(ti=ti, mc=mc, woh=woh, g=g, pr=pr):
                                    if "w" not in woh:
                                        woh["w"] = wslot(dr["wa"][j, g * 7 + 3 + pr * 3])
                                    ws_, wr_ = woh["w"]
                                    wv = ws_.rearrange("p (k c) -> p k c", k=2)
                                    c0, n = tiles[ti]
                                    pb = ps_rot(0, 4)
                                    mm(PSB[pb][:, 0:n], [(wv[:, h2, mc * 128:(mc + 1) * 128], OG[:, h2, c0:c0 + n]) for h2 in range(2)],
                                       reads=[wr_, ("OG", 0, ti), ("OG", 1, ti)], writes=[("ps", pb)])
                                    if ti == 0:
                                        dst = C[:, mc, :]
                                        dres = CR[mc]
                                        gt = MODG[:, l, mc, 2:3]
                                    else:
                                        dst = X[:, mc, (ti - 1) * TT: ti * TT]
                                        dres = XR[ti - 1][mc]
                                        gt = MODG[:, l, mc, b:b + 1]
                                    if mc % 4 == 2:
                                        _, tu, tur = tf_next()
                                        P.op("act", (lambda e: e.activation(out=tu[:, 0:n], in_=PSB[pb][:, 0:n], func=AF.Identity, scale=gt)),
                                             reads=[("ps", pb)], writes=[tur])
                                        P.op("pool", (lambda e: e.tensor_tensor(out=dst, in0=dst, in1=tu[:, 0:n], op=ALU.add)),
                                             reads=[tur, dres], writes=[dres])
                                    else:
                                        P.op("dve", (lambda e: e.scalar_tensor_tensor(out=dst, in0=PSB[pb][:, 0:n], scalar=gt, in1=dst, op0=ALU.mult, op1=ALU.add)),
                                             reads=[("ps", pb), dres], writes=[dres])
                                wout_q.append(emit_wout1)
                pop_wout(10 ** 6)
                P.barrier()

            def conv_layer(b, l):
                j = l // 2
                ctx_out = any(k % 2 == 0 for k in range(l + 1, 4))
                ranges = ([("ctx", 0, CL)] if ctx_out else []) + [("lat", 0, 1024), ("lat", 1024, 1024)]
                for (kind, r0, rl) in ranges:
                    row = 2 if kind == "ctx" else b
                    halo_r = (kind == "lat" and r0 == 0)
                    halo_l = (kind == "lat" and r0 > 0)
                    rt = []
                    if kind == "ctx":
                        rt.append((0, CL, C[:, :, :], CR))
                    else:
                        for i in range(2):
                            t = r0 // TT + i
                            rt.append((i * TT, TT, X[:, :, t * TT:(t + 1) * TT], XR[t]))
                        if halo_r:
                            rt.append((1024, 15, X[:, :, 1024:1039], XR[2]))
                    HR = [[("H", kc, ti) for kc in range(8)] for ti in range(len(rt))]
                    for ti, (c0, n, sv, sr) in enumerate(rt):
                        norm_mod(sv, sr, n, h_cv[:, :, c0:c0 + n], HR[ti], l, row)
                    outt = [x for x in rt if x[1] > 15]
                    nst = len(outt)
                    def p1_front(cc):
                            ws_, wr_ = wslot(dr["wc"][j, cc])
                            wv = ws_.rearrange("p (k c) -> p k c", k=8)
                            ub = P.rotate("ub", 2)
                            U = UB[:, ub, :]
                            ures = ("U", ub)
                            dww = smv(O_DWW + (j * 16 + cc) * 31, 31)
                            P.op("dve", (lambda e, ub=ub, dww=dww: e.tensor_tensor(out=DG[:, ub, :, :], in0=_bc_mid(IDN, 31), in1=_bc_last(dww, 128), op=ALU.mult)),
                                 reads=[], writes=[("DG", ub)])
                            if halo_l:
                                P.op("pool", (lambda e, U=U, cc=cc: e.tensor_copy(out=U[:, 0:15], in_=USV[:, cc, 0:15])), reads=[("USV", cc)], writes=[ures])
                            else:
                                P.op("pool", (lambda e, U=U: e.memset(U[:, 0:15], 0.0)), writes=[ures])
                            if not halo_r:
                                P.op("pool", (lambda e, U=U, rl=rl: e.memset(U[:, 15 + rl:15 + rl + 15], 0.0)), reads=[ures], writes=[ures])
                            for ti, (c0, n, sv, sr) in enumerate(rt):
                                pa = ps_rot(0, 4)
                                mm(PSB[pa][:, 0:n], [(wv[:, kc, 0:128], h_cv[:, kc, c0:c0 + n]) for kc in range(8)], reads=[wr_] + HR[ti], writes=[("ps", pa)])
                                pg = ps_rot(0, 4)
                                mm(PSB[pg][:, 0:n], [(wv[:, kc, 128:256], h_cv[:, kc, c0:c0 + n]) for kc in range(8)], reads=[wr_] + HR[ti], writes=[("ps", pg)])
                                _, t1, t1r = tf_next()
                                hbg = HB[:, j * 48 + 16 + cc: j * 48 + 16 + cc + 1]
                                hba = HB[:, j * 48 + cc: j * 48 + cc + 1]
                                P.op("act", (lambda e, pg=pg, n=n, t1=t1, hbg=hbg: e.activation(out=t1[:, 0:n], in_=PSB[pg][:, 0:n], func=AF.Tanh, scale=0.5, bias=hbg)),
                                     reads=[("ps", pg)], writes=[t1r])
                                _, t2, t2r = tf_next()
                                P.op("act", (lambda e, pa=pa, n=n, t2=t2, hba=hba: e.activation(out=t2[:, 0:n], in_=PSB[pa][:, 0:n], func=AF.Identity, scale=0.5, bias=hba)),
                                     reads=[("ps", pa)], writes=[t2r])
                                P.op("dve", (lambda e, n=n, t1=t1, t2=t2, U=U, c0=c0: e.scalar_tensor_tensor(out=U[:, 15 + c0:15 + c0 + n], in0=t1[:, 0:n], scalar=1.0, in1=t2[:, 0:n], op0=ALU.add, op1=ALU.mult)),
                                     reads=[t1r, t2r, ures], writes=[ures])
                            if halo_r:
                                P.op("pool", (lambda e, U=U, cc=cc: e.tensor_copy(out=USV[:, cc, 0:15], in_=U[:, 15 + 1009:15 + 1024])), reads=[ures], writes=[("USV", cc)])
                            return ub, U, ures

                    def p1_back(cc, ub, U, ures):
                            dwb = smv(O_DWB + j * 16 + cc, 1)
                            accs = []
                            for si, (c0, n, sv, sr) in enumerate(outt):
                                if NDT > 0:
                                    _, ac, acr = ct_next()
                                else:
                                    ac, acr = None, None
                                accs.append((ac, acr))
                            for k in range(NDT):
                                wk = smv(O_DWW + (j * 16 + cc) * 31 + k, 1)
                                for si, (c0, n, sv, sr) in enumerate(outt):
                                    ac, acr = accs[si]
                                    if k == 0:
                                        P.op("dve", (lambda e, ac=ac, n=n, U=U, c0=c0, k=k, wk=wk: e.tensor_scalar(out=ac[:, 0:n], in0=U[:, c0 + k:c0 + k + n], scalar1=wk, scalar2=None, op0=ALU.mult)),
                                             reads=[ures], writes=[acr])
                                    else:
                                        P.op("dve", (lambda e, ac=ac, n=n, U=U, c0=c0, k=k, wk=wk: e.scalar_tensor_tensor(out=ac[:, 0:n], in0=U[:, c0 + k:c0 + k + n], scalar=wk, in1=ac[:, 0:n], op0=ALU.mult, op1=ALU.add)),
                                             reads=[ures, acr], writes=[acr])
                            for si, (c0, n, sv, sr) in enumerate(outt):
                                ac, acr = accs[si]
                                pv = ps_rot(0, 4)
                                mm(PSB[pv][:, 0:n], [(DG[:, ub, k, :], U[:, c0 + k:c0 + k + n]) for k in range(NDT, 31)], reads=[("DG", ub), ures], writes=[("ps", pv)])
                                if NDT > 0:
                                    P.op("dve", (lambda e, pv=pv, n=n, ac=ac: e.tensor_tensor(out=ac[:, 0:n], in0=PSB[pv][:, 0:n], in1=ac[:, 0:n], op=ALU.add)),
                                         reads=[("ps", pv), acr], writes=[acr])
                                    vsrc = ac
                                    vres = acr
                                else:
                                    vsrc = PSB[pv]
                                    vres = ("ps", pv)
                                P.op("act", (lambda e, vsrc=vsrc, n=n, cc=cc, c0=c0, dwb=dwb: e.activation(out=VC[:, cc, c0:c0 + n], in_=vsrc[:, 0:n], func=AF.Identity, bias=dwb)),
                                     reads=[vres], writes=[("VC", cc, si)])
                                _, sq, sqr = tb_next()
                                P.op("act", (lambda e, vsrc=vsrc, n=n, sq=sq, dwb=dwb: e.activation(out=sq[:, 0:n], in_=vsrc[:, 0:n], func=AF.Square, bias=dwb)),
                                     reads=[vres], writes=[sqr])
                                mm(PSB[4 + si][:, 0:n], [(ONES[:, :], VC[:, cc, c0:c0 + n])], reads=[("VC", cc, si)], writes=[("ps", 4 + si)], flags=(cc == 0, cc == 15))
                                mm(PSB[6 + si][:, 0:n], [(ONES[:, :], sq[:, 0:n])], reads=[sqr], writes=[("ps", 6 + si)], flags=(cc == 0, cc == 15))

                    fctx = p1_front(0)
                    for cc in range(16):
                        nxt = p1_front(cc + 1) if cc + 1 < 16 else None
                        p1_back(cc, *fctx)
                        fctx = nxt
                    for si, (c0, n, sv, sr) in enumerate(outt):
                        P.op("dve", (lambda e, si=si, c0=c0, n=n: e.tensor_scalar(out=S1SB[:, c0:c0 + n], in0=PSB[4 + si][:, 0:n], scalar1=1.0 / 2048, scalar2=None, op0=ALU.mult)),
                             reads=[("ps", 4 + si)], writes=[("S1SB", si)])
                        _, t1, t1r = tf_next()
                        P.op("dve", (lambda e, c0=c0, n=n, t1=t1: e.tensor_tensor(out=t1[:, 0:n], in0=S1SB[:, c0:c0 + n], in1=S1SB[:, c0:c0 + n], op=ALU.mult)),
                             reads=[("S1SB", si)], writes=[t1r])
                        _, t2, t2r = tf_next()
                        P.op("dve", (lambda e, si=si, n=n, t1=t1, t2=t2: e.scalar_tensor_tensor(out=t2[:, 0:n], in0=PSB[6 + si][:, 0:n], scalar=1.0 / 2048, in1=t1[:, 0:n], op0=ALU.mult, op1=ALU.subtract)),
                             reads=[("ps", 6 + si), t1r], writes=[t2r])
                        _, t3, t3r = tf_next()
                        P.op("act", (lambda e, n=n, t2=t2, t3=t3: e.activation(out=t3[:, 0:n], in_=t2[:, 0:n], func=AF.Sqrt, bias=EPSD[:, 1:2])),
                             reads=[t2r], writes=[t3r])
                        P.op("dve", (lambda e, c0=c0, n=n, t3=t3: e.reciprocal(out=RSTD[:, c0:c0 + n], in_=t3[:, 0:n])),
                             reads=[t3r], writes=[("RSTD", si)])
                    cpend = []
                    s2pend = []
                    dg_guard.clear()
                    for pz in range(8):
                        zs, zr = wslot(dr["wc"][j, 16 + pz * 2])
                        zv = zs.rearrange("p (k c) -> p k c", k=8)
                        woh = {}
                        for si, (c0, n, sv, sr) in enumerate(outt):
                            keep = []
                            for tag, f in cpend:
                                if tag == si:
                                    f()
                                else:
                                    keep.append((tag, f))
                            cpend[:] = keep
                            for ci in range(2):
                                cc = pz * 2 + ci
                                bz = smv(O_CB + j * 48 + 32 + cc, 1)
                                lng = smv(O_LNG + j * 16 + cc, 1)
                                lnb = smv(O_LNB + j * 16 + cc, 1)
                                hbz = HB[:, j * 48 + 32 + cc: j * 48 + 32 + cc + 1]
                                hlg = HLN[:, j * 16 + cc: j * 16 + cc + 1]
                                hlb = HLN[:, 32 + j * 16 + cc: 32 + j * 16 + cc + 1]
                                tB, tBr, xB = ct2_next()
                                tC, tCr, xC = ct2_next()
                                P.op("dve", (lambda e, n=n, tB=tB, cc=cc, c0=c0: e.tensor_tensor(out=tB[:, 0:n], in0=VC[:, cc, c0:c0 + n], in1=S1SB[:, c0:c0 + n], op=ALU.subtract)),
                                     reads=[("VC", cc, si), ("S1SB", si)], writes=[tBr] + xB)
                                P.op("pool", (lambda e, n=n, tB=tB, c0=c0: e.tensor_tensor(out=tB[:, 0:n], in0=tB[:, 0:n], in1=RSTD[:, c0:c0 + n], op=ALU.mult)),
                                     reads=[tBr, ("RSTD", si)], writes=[tBr])
                                P.op("act", (lambda e, n=n, tB=tB, tC=tC, hlg=hlg, hlb=hlb: e.activation(out=tC[:, 0:n], in_=tB[:, 0:n], func=AF.Tanh, scale=hlg, bias=hlb)),
                                     reads=[tBr], writes=[tCr] + xC)
                                P.op("act", (lambda e, n=n, tB=tB, lng=lng, lnb=lnb: e.activation(out=tB[:, 0:n], in_=tB[:, 0:n], func=AF.Identity, scale=lng, bias=lnb)),
                                     reads=[tBr], writes=[tBr])

                                def stage2(n=n, c0=c0, si=si, ci=ci, cc=cc, tB=tB, tBr=tBr, tC=tC, tCr=tCr, bz=bz, hbz=hbz, zv=zv, zr=zr, HRs=HR[si]):
                                    tA, tAr, xA = ct2_next()
                                    tD, tDr, xD = ct2_next()
                                    P.op("dve", (lambda e: e.scalar_tensor_tensor(out=tC[:, 0:n], in0=tC[:, 0:n], scalar=1.0, in1=tB[:, 0:n], op0=ALU.add, op1=ALU.mult)),
                                         reads=[tBr, tCr], writes=[tCr])
                                    pzb = ps_rot(0, 8)
                                    mm(PSB[pzb][:, 0:n], [(zv[:, kc, ci * 128:(ci + 1) * 128], h_cv[:, kc, c0:c0 + n]) for kc in range(8)], reads=[zr] + HRs, writes=[("ps", pzb)])
                                    P.op("act", (lambda e: e.activation(out=tA[:, 0:n], in_=PSB[pzb][:, 0:n], func=AF.Identity, bias=bz)),
                                         reads=[("ps", pzb)], writes=[tAr] + xA)
                                    P.op("act", (lambda e: e.activation(out=tD[:, 0:n], in_=PSB[pzb][:, 0:n], func=AF.Tanh, scale=0.5, bias=hbz)),
                                         reads=[("ps", pzb)], writes=[tDr] + xD)
                                    P.op("dve", (lambda e: e.scalar_tensor_tensor(out=tA[:, 0:n], in0=tD[:, 0:n], scalar=1.0, in1=tA[:, 0:n], op0=ALU.add, op1=ALU.mult)),
                                         reads=[tAr, tDr], writes=[tAr])
                                    P.op("pool", (lambda e: e.tensor_tensor(out=NG[:, ci, c0:c0 + n], in0=tC[:, 0:n], in1=tA[:, 0:n], op=ALU.mult)),
                                         reads=[tAr, tCr], writes=[("NG", ci, si)])
                                    if cpend:
                                        cpend.pop(0)[1]()
                                for f in s2pend:
                                    f()
                                del s2pend[:]
                                s2pend.append(stage2)
                            for f in s2pend:
                                f()
                            del s2pend[:]
                            def emit_cwout(mcs, si=si, c0=c0, n=n, sv=sv, sr=sr, woh=woh, pz=pz):
                                if "w" not in woh:
                                    woh["w"] = wslot(dr["wc"][j, 16 + pz * 2 + 1])
                                ws_, wr_ = woh["w"]
                                wv = ws_.rearrange("p (k c) -> p k c", k=2)
                                for mc in mcs:
                                    pb = ps_rot(0, 8)
                                    mm(PSB[pb][:, 0:n], [(wv[:, ci2, mc * 128:(mc + 1) * 128], NG[:, ci2, c0:c0 + n]) for ci2 in range(2)],
                                       reads=[wr_, ("NG", 0, si), ("NG", 1, si)], writes=[("ps", pb)])
                                    dst = sv[:, mc, :]
                                    gt = MODG[:, l, mc, row:row + 1]
                                    if mc % 4 == 2:
                                        tu, tur, xu = ct2_next()
                                        P.op("act", (lambda e, pb=pb, n=n, tu=tu, gt=gt: e.activation(out=tu[:, 0:n], in_=PSB[pb][:, 0:n], func=AF.Identity, scale=gt)),
                                             reads=[("ps", pb)], writes=[tur] + xu)
                                        P.op("pool", (lambda e, n=n, tu=tu, dst=dst: e.tensor_tensor(out=dst, in0=dst, in1=tu[:, 0:n], op=ALU.add)),
                                             reads=[tur, sr[mc]], writes=[sr[mc]])
                                    else:
                                        P.op("dve", (lambda e, pb=pb, n=n, dst=dst, gt=gt: e.scalar_tensor_tensor(out=dst, in0=PSB[pb][:, 0:n], scalar=gt, in1=dst, op0=ALU.mult, op1=ALU.add)),
                                             reads=[("ps", pb), sr[mc]], writes=[sr[mc]])
                            cpend.append((si, lambda f=emit_cwout: f(range(0, 4))))
                            cpend.append((si, lambda f=emit_cwout: f(range(4, 8))))
                    for tag, f in cpend:
                        f()
                    del cpend[:]
                    P.barrier()

            for b in range(nb):
                if b > 0:
                    load_x(b)
                try:
                    for l in range(n_layers):
                        if l % 2 == 0:
                            attn_layer(b, l)
                        else:
                            conv_layer(b, l)
                except _Stop:
                    P.barrier()
                    P.dma("sp", "st", [(dr["dH"], H[:, :]), (dr["dLS"], LS[:, :]), (dr["dLSF"], LSF[:, :]),
                                       (dr["dMOD"][:, 0, :], MODA[:, :, :, :].rearrange("p a b c -> p (a b c)")),
                                       (dr["dMOD"][:, 1, :], MODS[:, :, :, :].rearrange("p a b c -> p (a b c)")),
                                       (dr["dMOD"][:, 2, :], MODG[:, :, :, :].rearrange("p a b c -> p (a b c)"))])
                if final_norm:
                    for t in range(4):
                        xs = X[:, :, t * TT:(t + 1) * TT]
                        norm_mod(xs, XR[t], TT, xs, XR[t], None, None)
                P.dma("sp", "st", [(dr["outT"][b, kc], X[:, kc, :]) for kc in range(8)]
                      + ([(dr["cxo"][b, kc], C[:, kc, :]) for kc in range(8)] if "cxo" in dr else []),
                      reads=[r for t in range(4) for r in XR[t]] + CR)
            P.barrier()

        Pd = Prog(dry=True)
        run(Pd)
        P = Prog(dry=False)
        P.wlist = Pd.wsrc
        run(P)
        global _LAST_PROG
        _LAST_PROG = P

        with nc.Block() as block:
            def replay(eng, e):
                for waits, fn, semkey, inc in P.ops[eng]:
                    for k, v in waits:
                        e.wait_ge(sems[k], v)
                    if fn is not None:
                        ins = fn(e)
                        ins.then_inc(sems[semkey], inc)

            @block.tensor
            def _(e):
                replay("pe", e)

            @block.scalar
            def _(e):
                replay("act", e)

            @block.vector
            def _(e):
                replay("dve", e)

            @block.gpsimd
            def _(e):
                replay("pool", e)

            @block.sync
            def _(e):
                replay("sp", e)
    return nc


def _rope_tables():
    rows = S // 64
    row = np.repeat(np.arange(rows), 64).astype(np.float32)
    col = np.tile(np.arange(64), rows).astype(np.float32)
    n_axis = HD // 4
    inv = (10000.0 ** (-np.arange(n_axis, dtype=np.float32) / n_axis)).astype(np.float32)
    ang = np.concatenate([row[:, None] * inv, col[:, None] * inv], axis=-1).astype(np.float32)
    cos = np.cos(ang).astype(np.float32).T
    sin = np.sin(ang).astype(np.float32).T
    return np.ascontiguousarray(np.stack([np.concatenate([cos, cos], 0), np.concatenate([sin, sin], 0)], 0))


def _const16():
    rm = np.zeros((128, 128), np.float32)
    for m in range(64):
        rm[m + 64, m] = -1.0
        rm[m, m + 64] = 1.0
    b = np.arange(128)[:, None]
    a = np.arange(128)[None, :]
    mprev = np.where(b >= a, 0.0, -30000.0).astype(np.float32)
    mnext = np.where(b <= a, 0.0, -30000.0).astype(np.float32)
    idn = np.eye(128, dtype=np.float32)
    return np.ascontiguousarray(np.stack([rm, mprev, mnext, idn], 1).reshape(128, 512))


def _pk(v):
    return np.ascontiguousarray(v.reshape(-1, 128).T)


def _wslot_k8(w, cols):
    s = w[:, cols].reshape(8, 128, len(cols)).transpose(1, 0, 2)
    return s.reshape(128, -1)


def _wslot_out(w, r0):
    s = w[r0:r0 + 256].reshape(2, 128, 1024).transpose(1, 0, 2)
    return s.reshape(128, -1)


def _prep_shared(inp):
    f = np.float32
    ada_w = np.asarray(inp["ada_w"], f)
    adaw = np.ascontiguousarray(ada_w.reshape(4, 8, 128, 12, 256).transpose(0, 3, 2, 1, 4).reshape(4, 12, 128, 2048))
    small = np.zeros((128, NS), f)
    for l in range(4):
        small[:, O_ADAB + l * 24:O_ADAB + (l + 1) * 24] = _pk(np.asarray(inp["ada_b"], f)[l])
        small[:, O_NG + l * 8:O_NG + (l + 1) * 8] = _pk(np.asarray(inp["norm_g"], f)[l])
    small[:, O_FG:O_FG + 8] = _pk(np.asarray(inp["final_g"], f))
    small[:, O_SINK:O_SINK + 32] = np.asarray(inp["attn_sink"], f).reshape(1, 32)
    for j in range(2):
        small[:, O_CB + j * 48:O_CB + (j + 1) * 48] = _pk(np.asarray(inp["conv_b_in"], f)[j])
        dw = np.asarray(inp["conv_dw_w"], f)[j]
        small[:, O_DWW + j * 496:O_DWW + (j + 1) * 496] = dw.T.reshape(16, 128, 31).transpose(1, 0, 2).reshape(128, 496)
        small[:, O_DWB + j * 16:O_DWB + (j + 1) * 16] = _pk(np.asarray(inp["conv_dw_b"], f)[j])
        small[:, O_LNG + j * 16:O_LNG + (j + 1) * 16] = _pk(np.asarray(inp["conv_ln_g"], f)[j])
        small[:, O_LNB + j * 16:O_LNB + (j + 1) * 16] = _pk(np.asarray(inp["conv_ln_b"], f)[j])
    wa = np.zeros((2, 28, 128, 2048), f)
    awi = np.asarray(inp["attn_w_in"], f)
    awo = np.asarray(inp["attn_w_out"], f)
    for j in range(2):
        for g in range(4):
            kvc = list(range(2048 + g * 128, 2048 + (g + 1) * 128)) + list(range(2560 + g * 128, 2560 + (g + 1) * 128))
            wa[j, g * 7 + 0] = _wslot_k8(awi[j], kvc)
            for pr in range(2):
                q0 = g * 512 + pr * 256
                wa[j, g * 7 + 1 + pr * 3] = _wslot_k8(awi[j], list(range(q0, q0 + 256)))
                wa[j, g * 7 + 2 + pr * 3] = _wslot_k8(awi[j], list(range(3072 + q0, 3072 + q0 + 256)))
                wa[j, g * 7 + 3 + pr * 3] = _wslot_out(awo[j], q0)
    wc = np.zeros((2, 32, 128, 2048), f)
    cwi = np.asarray(inp["conv_w_in"], f)
    cwo = np.asarray(inp["conv_w_out"], f)
    for j in range(2):
        for cc in range(16):
            cols = list(range(cc * 128, (cc + 1) * 128)) + list(range(2048 + cc * 128, 2048 + (cc + 1) * 128))
            wc[j, cc] = _wslot_k8(cwi[j], cols)
        for pz in range(8):
            wc[j, 16 + pz * 2] = _wslot_k8(cwi[j], list(range(4096 + pz * 256, 4096 + (pz + 1) * 256)))
            wc[j, 16 + pz * 2 + 1] = _wslot_out(cwo[j], pz * 256)
    return {"adaw": adaw, "small": small, "cb16": _const16(), "rope": _rope_tables(), "wa": wa, "wc": wc}


def _prep_core(inp, shared, core):
    f = np.float32
    b0 = core * BPC
    x = np.asarray(inp["x"], f)[b0:b0 + BPC]
    ctx = np.asarray(inp["ctx"], f)[b0:b0 + BPC]
    xT = np.ascontiguousarray(x.transpose(0, 2, 1).reshape(BPC, 8, 128, S))
    cxT = np.ascontiguousarray(ctx.transpose(0, 2, 1).reshape(BPC, 8, 128, CL))
    c = np.asarray(inp["c"], f)[b0:b0 + BPC]
    rows = np.concatenate([c, np.asarray(inp["c_ctx"], f)[None, :]], 0)
    cT = np.ascontiguousarray(rows.reshape(3, 8, 128).transpose(2, 1, 0))
    m = {"xT": xT, "cxT": cxT, "cT": cT}
    m.update(shared)
    return m


_NC_CACHE = {}


def kernel(**inputs):
    if "nc" not in _NC_CACHE:
        _NC_CACHE["nc"] = build_program()
    nc = _NC_CACHE["nc"]
    shared = _prep_shared(inputs)
    in_maps = [_prep_core(inputs, shared, c) for c in range(NCORES)]
    res = run_bass_kernel_spmd(nc, in_maps, core_ids=list(range(NCORES)))
    outs = []
    for c in range(NCORES):
        o = np.asarray(res.results[c]["outT"], np.float32).reshape(BPC, D, S)
        outs.append(o.transpose(0, 2, 1))
    return np.ascontiguousarray(np.concatenate(outs, 0)).astype(np.float32)
```
